# Optimizing a Trainium2 kernel written in Bass

```python
import math
import jax, jax.numpy as jnp
from jax import lax
import numpy as np

D_MODEL = 1024
BATCH = 2
SEQ = 8192
DEPTH = 1

CHUNK = 64
N_ATTN_HEADS = 8
HEAD_DIM = 64
ATTN_WIDTH = N_ATTN_HEADS * HEAD_DIM
CONV_GROUPS = 8
CONV_WIDTH_CH = 512
CONV_K = 3
Q_BLOCK = 128
FFN_HIDDEN = int(math.ceil(8 * D_MODEL / 3 / 256) * 256)
RMS_EPS = 1e-6

_COLS = [ATTN_WIDTH, ATTN_WIDTH, ATTN_WIDTH, N_ATTN_HEADS,
         CONV_WIDTH_CH, CONV_WIDTH_CH, CONV_WIDTH_CH,
         D_MODEL, D_MODEL]
IN_COLS = sum(_COLS)
_SPLITS = list(np.cumsum(_COLS)[:-1])

kernel_name = "hybrid_shortconv_fox_sandwich_block"


def rmsnorm(x, g):
    xf = x.astype(jnp.float32)
    y = xf * lax.rsqrt(jnp.mean(xf * xf, axis=-1, keepdims=True) + RMS_EPS)
    return (y * g.astype(jnp.float32)).astype(x.dtype)


def causal_depthwise_conv(z, w):
    c = z.shape[-1]
    return lax.conv_general_dilated(
        z, w[:, None, :].astype(z.dtype), window_strides=(1,),
        padding=[(CONV_K - 1, 0)], dimension_numbers=("NWC", "WIO", "NWC"),
        feature_group_count=c)


def forgetting_attention(q, k, v, logf):
    s_len = q.shape[1]
    scale = 1.0 / math.sqrt(q.shape[-1])
    c = jnp.cumsum(logf, axis=1).transpose(0, 2, 1)
    outs = []
    for s0 in range(0, s_len, Q_BLOCK):
        s1 = s0 + Q_BLOCK
        logits = jnp.einsum("bqhd,bkhd->bhqk", q[:, s0:s1], k[:, :s1],
                            preferred_element_type=jnp.float32) * scale
        logits = logits + c[:, :, s0:s1, None] - c[:, :, None, :s1]
        qpos = jnp.arange(s0, s1)[:, None]
        kpos = jnp.arange(s1)[None, :]
        logits = jnp.where(kpos <= qpos, logits, -jnp.inf)
        p = jax.nn.softmax(logits, axis=-1)
        outs.append(jnp.einsum("bhqk,bkhd->bqhd", p.astype(v.dtype), v[:, :s1]))
    return jnp.concatenate(outs, axis=1)


def mixer(h, w_in, b_f, conv_w, w_branch, w_out):
    bsz, s_len, _ = h.shape
    proj = jnp.einsum("bsd,dn->bsn", h, w_in)
    q, k, v, f_logit, g_b, g_c, hc, gate_conv, gate_attn = jnp.split(proj, _SPLITS, axis=-1)
    ya = g_b * causal_depthwise_conv(g_c * hc, conv_w)
    shp = (bsz, s_len, N_ATTN_HEADS, HEAD_DIM)
    logf = jax.nn.log_sigmoid(f_logit.astype(jnp.float32) + b_f.astype(jnp.float32))
    yb = forgetting_attention(q.reshape(shp), k.reshape(shp), v.reshape(shp), logf)
    yb = yb.reshape(bsz, s_len, ATTN_WIDTH)
    ya_d = jnp.einsum("bsc,cd->bsd", ya, w_branch[0])
    yb_d = jnp.einsum("bsc,cd->bsd", yb, w_branch[1])
    merged = jax.nn.sigmoid(gate_conv) * ya_d + jax.nn.sigmoid(gate_attn) * yb_d
    return jnp.einsum("bsd,de->bse", merged, w_out)


def swiglu(h, w_gate_up, w_down):
    gu = jnp.einsum("bsd,df->bsf", h, w_gate_up)
    g, u = jnp.split(gu, 2, axis=-1)
    return jnp.einsum("bsf,fd->bsd", jax.nn.silu(g) * u, w_down)


def setup_inputs(seed: int = 0) -> dict:
    key = jax.random.key(seed)
    ks = jax.random.split(key, 12)
    f32 = jnp.float32

    def nrm(k, shape, fan_in):
        return jax.random.normal(k, shape, f32) * (fan_in ** -0.5)

    def gain(k):
        return 1.0 + 0.05 * jax.random.normal(k, (DEPTH, D_MODEL), f32)

    return {
        "x": jax.random.normal(ks[0], (BATCH, SEQ, D_MODEL), f32),
        "norm_mix_pre": gain(ks[1]),
        "norm_mix_post": gain(ks[2]),
        "w_in": nrm(ks[3], (DEPTH, D_MODEL, IN_COLS), D_MODEL),
        "b_f": 2.0 + 0.5 * jax.random.normal(ks[4], (DEPTH, N_ATTN_HEADS), f32),
        "conv_w": nrm(ks[5], (DEPTH, CONV_K, CONV_WIDTH_CH), CONV_K),
        "w_branch": nrm(ks[6], (DEPTH, 2, ATTN_WIDTH, D_MODEL), ATTN_WIDTH),
        "w_out": nrm(ks[7], (DEPTH, D_MODEL, D_MODEL), D_MODEL),
        "norm_ffn_pre": gain(ks[8]),
        "norm_ffn_post": gain(ks[9]),
        "w_gate_up": nrm(ks[10], (DEPTH, D_MODEL, 2 * FFN_HIDDEN), D_MODEL),
        "w_down": nrm(ks[11], (DEPTH, FFN_HIDDEN, D_MODEL), FFN_HIDDEN),
    }


def reference(x, norm_mix_pre, norm_mix_post, w_in, b_f, conv_w, w_branch, w_out,
              norm_ffn_pre, norm_ffn_post, w_gate_up, w_down):
    for l in range(DEPTH):
        h = rmsnorm(x, norm_mix_pre[l])
        m = mixer(h, w_in[l], b_f[l], conv_w[l], w_branch[l], w_out[l])
        x = x + rmsnorm(m, norm_mix_post[l])
        h = rmsnorm(x, norm_ffn_pre[l])
        f = swiglu(h, w_gate_up[l], w_down[l])
        x = x + rmsnorm(f, norm_ffn_post[l])
    return x
```

```python
import numpy as np
from contextlib import ExitStack
import concourse.bass as bass
import concourse.mybir as mybir
from concourse.bass_utils import run_bass_kernel_spmd

F32 = mybir.dt.float32
BF16 = mybir.dt.bfloat16
AF = mybir.ActivationFunctionType
ALU = mybir.AluOpType

D = 1024
SEQ = 8192
NB = 2
FFN = 2816
NEG = -30000.0
ARENA_ELEMS = 106368
DEBUG = False

ENG = ("pe", "act", "dve", "pool", "sp")


class Buf:
    __slots__ = ("w", "r")

    def __init__(self):
        self.w = None
        self.r = {}


class Builder:
    def __init__(self, nc, stack):
        self.nc = nc
        self.stack = stack
        self.ops = {e: [] for e in ENG}
        self.cnt = {}
        self.sem = {}
        self.waited = {e: {} for e in ENG}
        self.phase = 0

    def getsem(self, key):
        if key not in self.sem:
            self.sem[key] = self.stack.enter_context(self.nc.semaphore(key))
            self.cnt[key] = 0
        return self.sem[key]

    def _wait(self, eng, ev):
        if ev is None:
            return
        key, n = ev
        if eng == "pe" and key.startswith("pe_"):
            return
        if self.waited[eng].get(key, 0) >= n:
            return
        self.waited[eng][key] = n
        sem = self.sem[key]
        self.ops[eng].append(lambda E: E.wait_ge(sem, n))

    def _hazards(self, eng, reads, writes, deps):
        for b in reads:
            self._wait(eng, b.w)
        for b in writes:
            self._wait(eng, b.w)
            for k, n in b.r.items():
                self._wait(eng, (k, n))
        for d in deps:
            self._wait(eng, d)

    def _record(self, ev, reads, writes):
        for b in reads:
            if b.r.get(ev[0], 0) < ev[1]:
                b.r[ev[0]] = ev[1]
        for b in writes:
            b.w = ev
            b.r = {}

    def emit(self, eng, fn, reads=(), writes=(), deps=(), sig=True):
        self._hazards(eng, reads, writes, deps)
        key = "%s_%d" % (eng, self.phase)
        sem = self.getsem(key)
        if sig:
            self.cnt[key] += 1
            n = self.cnt[key]
            self.ops[eng].append(lambda E: fn(E).then_inc(sem, 1))
        else:
            n = self.cnt[key] + 1
            self.ops[eng].append(lambda E: fn(E))
        ev = (key, n)
        self._record(ev, reads, writes)
        return ev

    def dma(self, out, in_, semkey, reads=(), writes=(), deps=(), q="sp"):
        self._hazards(q, reads, writes, deps)
        sem = self.getsem(semkey)
        self.cnt[semkey] += 16
        n = self.cnt[semkey]
        self.ops[q].append(lambda E: E.dma_start(out=out, in_=in_).then_inc(sem, 16))
        ev = (semkey, n)
        self._record(ev, reads, writes)
        return ev

    def barrier(self, exclude=()):
        evs = [(k, n) for k, n in self.cnt.items() if n > 0 and k not in exclude]
        for e in ENG:
            for ev in evs:
                self._wait(e, ev)
        self.phase += 1

    def mm(self, out, lhsT, rhs, start, stop, reads=(), writes=(), deps=(), sig=False):
        return self.emit("pe", lambda E: E.matmul(out, lhsT=lhsT, rhs=rhs, start=start, stop=stop),
                         reads, writes, deps, sig)

    def act(self, out, in_, func, bias=None, scale=None, reads=(), writes=(), deps=()):
        kw = {}
        if bias is not None:
            kw["bias"] = bias
        if scale is not None:
            kw["scale"] = scale
        return self.emit("act", lambda E: E.activation(out=out, in_=in_, func=func, **kw), reads, writes, deps)

    def tt(self, eng, out, in0, in1, op, reads=(), writes=(), deps=()):
        return self.emit(eng, lambda E: E.tensor_tensor(out=out, in0=in0, in1=in1, op=op), reads, writes, deps)

    def ts(self, eng, out, in0, s1, op0, s2=None, op1=None, reads=(), writes=(), deps=()):
        if op1 is None:
            return self.emit(eng, lambda E: E.tensor_scalar(out=out, in0=in0, scalar1=s1, scalar2=None, op0=op0),
                             reads, writes, deps)
        return self.emit(eng, lambda E: E.tensor_scalar(out=out, in0=in0, scalar1=s1, scalar2=s2, op0=op0, op1=op1),
                         reads, writes, deps)

    def stt(self, eng, out, in0, scalar, in1, op0, op1, reads=(), writes=(), deps=()):
        return self.emit(eng, lambda E: E.scalar_tensor_tensor(out=out, in0=in0, scalar=scalar, in1=in1, op0=op0, op1=op1),
                         reads, writes, deps)

    def cp(self, eng, out, in_, reads=(), writes=(), deps=()):
        return self.emit(eng, lambda E: E.tensor_copy(out=out, in_=in_), reads, writes, deps)

    def memset(self, eng, ap, val, writes=()):
        return self.emit(eng, lambda E: E.memset(ap, val), (), writes, ())


def build_program():
    nc = bass.Bass("TRN2", target_bir_lowering=False)
    dt = nc.dram_tensor
    xc = dt("xc", [16, 128, 8 * 512], F32, kind="ExternalInput").ap()
    xo = dt("xo", [4, 128, 8 * 512], F32, kind="ExternalInput").ap()
    xh = dt("xh", [128, 8 * 32], F32, kind="ExternalInput").ap()
    w_in = dt("w_in", [D, 5128], F32, kind="ExternalInput").ap()
    w_br = dt("w_br", [2, 512, D], F32, kind="ExternalInput").ap()
    w_out = dt("w_out", [D, D], F32, kind="ExternalInput").ap()
    w_gu = dt("w_gu", [D, 2 * FFN], F32, kind="ExternalInput").ap()
    w_dn = dt("w_dn", [FFN, D], F32, kind="ExternalInput").ap()
    prm = dt("prm", [128, 48], F32, kind="ExternalInput").ap()
    NCST = 128 + 128 + 1024 + 256
    cst = dt("cst", [128, NCST], F32, kind="ExternalInput").ap()
    y = dt("y", [4, 128, 8 * 512], F32, kind="ExternalOutput").ap()
    if DEBUG:
        dbg_yb = dt("dbg_yb", [128, 4 * 2048], F32, kind="ExternalOutput").ap()
        dbg_x1 = dt("dbg_x1", [128, 8 * 2048], F32, kind="ExternalOutput").ap()

    w_in_r = w_in.rearrange("(fc p) n -> p fc n", p=128)
    w_gu_r = w_gu.rearrange("(fc p) n -> p fc n", p=128)
    w_out_r = w_out.rearrange("(fc p) n -> p fc n", p=128)
    w_dn_r = w_dn.rearrange("(hc p) n -> p hc n", p=128)

    with ExitStack() as stack:
        arena = stack.enter_context(nc.sbuf_tensor("arena", [128, ARENA_ELEMS], BF16))
        PS = [stack.enter_context(nc.psum_tensor("ps%d" % i, [128, 512], F32)) for i in range(8)]
        bld = Builder(nc, stack)
        B = bld

        def V(off, dtype, cols):
            assert off % 4 == 0
            if dtype == BF16:
                assert off // 2 + cols <= ARENA_ELEMS, (off, cols)
                return arena[:, off // 2: off // 2 + cols]
            assert off // 2 + 2 * cols <= ARENA_ELEMS, (off, cols)
            return arena[:, off // 2: off // 2 + 2 * cols].bitcast(F32)

        class Alloc:
            def __init__(self, base, limit):
                self.p = base
                self.limit = limit

            def __call__(self, dtype, cols):
                nb = cols * (2 if dtype == BF16 else 4)
                nb = (nb + 63) // 64 * 64
                off = self.p
                self.p += nb
                assert self.p <= self.limit, ("arena overflow", self.p, self.limit)
                return V(off, dtype, cols)

        TOP = ARENA_ELEMS * 2
        al = Alloc(0, TOP)
        prm_sb = al(F32, 48)
        prm_b = Buf()
        gains = prm_sb[:, 0:32]
        convw = prm_sb[:, 32:44]
        bfcol = prm_sb[:, 44:46]
        negbf = al(F32, 2)
        cst_f = al(F32, 128 + 256)
        identf = cst_f[:, 0:128]
        kvalid = cst_f[:, 128:384]
        onesf = al(F32, 128)
        identb = al(BF16, 128)
        trib = al(BF16, 128)
        selb = al(BF16, 1024)
        onesS = al(BF16, 128)
        cbuf = Buf()
        epsc = al(F32, 1)
        BASE0 = al.p
        ybT = al(BF16, 4 * 2048)
        ybT_b = [Buf() for _ in range(16)]
        BASE = al.p

        tmp = Alloc(BASE, TOP)
        cst_st = tmp(F32, NCST)
        st_b = Buf()
        B.dma(prm_sb, prm[:, :], "ld_c", writes=[prm_b])
        B.dma(cst_st, cst[:, :], "ld_c2", writes=[st_b])
        B.ts("dve", negbf, bfcol, -1.0, ALU.mult, reads=[prm_b], writes=[cbuf])
        B.cp("dve", cst_f[:, 0:128], cst_st[:, 0:128], reads=[st_b], writes=[cbuf])
        B.cp("dve", kvalid, cst_st[:, 1280:1536], reads=[st_b], writes=[cbuf])
        B.cp("dve", identb, cst_st[:, 0:128], reads=[st_b], writes=[cbuf])
        B.cp("dve", trib, cst_st[:, 128:256], reads=[st_b], writes=[cbuf])
        B.cp("dve", selb, cst_st[:, 256:1280], reads=[st_b], writes=[cbuf])
        B.memset("dve", onesf, 1.0, writes=[cbuf])
        B.memset("dve", onesS, 1.0 / 1024.0, writes=[cbuf])
        B.memset("dve", epsc, 1e-6, writes=[cbuf])
        B.barrier()

        EPS = 1e-6

        def rstd_from_ms(ms_ps, ms_b, lnt, lnt_b, rstd, rstd_b, ncols):
            B.act(lnt[:, 0:ncols], ms_ps[:, 0:ncols], AF.Ln, bias=epsc, reads=[ms_b], writes=[lnt_b])
            B.act(rstd[:, 0:ncols], lnt[:, 0:ncols], AF.Exp, scale=-0.5, reads=[lnt_b], writes=[rstd_b])


        pend = []

        def pump(n=1):
            for _ in range(n):
                if pend:
                    pend.pop(0)()

        def flush():
            while pend:
                pend.pop(0)()

        stg_b = [Buf(), Buf()]
        A1_bs = [Buf() for _ in range(8)]
        lw_k = [0]

        def load_w_steps(dst3, dst_bs, src3, ncols, gain_off, extra_w=(), dve_only=False):
            nfc = dst3.shape[1]
            steps = []
            for c in range(0, ncols, 256):
                w = min(256, ncols - c)

                def step(c=c, w=w):
                    sl = lw_k[0] % 2
                    lw_k[0] += 1
                    st3 = stg[sl][:, 0:nfc * w].rearrange("p (f n) -> p f n", n=w)
                    B.dma(st3, src3[:, :, c:c + w], "ld_st%d" % sl, writes=[stg_b[sl]])
                    for f in range(nfc):
                        wr = [dst_bs[f]] + list(extra_w)
                        if gain_off is None:
                            if f % 2 and not dve_only:
                                B.act(dst3[:, f, c:c + w], st3[:, f, :], AF.Copy, reads=[stg_b[sl]], writes=wr)
                            else:
                                B.cp("dve", dst3[:, f, c:c + w], st3[:, f, :], reads=[stg_b[sl]], writes=wr)
                        else:
                            gc = gains[:, gain_off + f:gain_off + f + 1]
                            if f % 2 and not dve_only:
                                B.act(dst3[:, f, c:c + w], st3[:, f, :], AF.Copy, scale=gc, reads=[stg_b[sl], prm_b], writes=wr)
                            else:
                                B.ts("dve", dst3[:, f, c:c + w], st3[:, f, :], gc, ALU.mult, reads=[stg_b[sl], prm_b], writes=wr)
                steps.append(step)
            return steps

        for p in range(2):
            a = Alloc(BASE, TOP)
            KT = a(BF16, 2 * 8192)
            VA = a(BF16, 64 * 2 * 192)
            QT = a(BF16, 4 * 2048)
            Wb = a(BF16, 8 * 776)
            WF = a(BF16, 8 * 128)
            XS_OFF = a.p
            xs = [a(F32, 4096) for _ in range(2)]
            xb = [a(BF16, 4096) for _ in range(2)]
            sq = a(BF16, 4096)
            gq = a(BF16, 2048)
            biasK = a(F32, 256)
            lnt = a(F32, 512)
            rstd = [a(F32, 512) for _ in range(2)]
            rcol = [a(F32, 4) for _ in range(2)]
            fr = a(F32, 512)
            ee = lnt
            cn = [a(F32, 512) for _ in range(2)]
            t0 = a(F32, 128)
            hib = a(BF16, 128)
            r1 = a(F32, 128)
            midb = a(BF16, 128)
            r2 = a(F32, 128)
            PT = [xs[0][:, 256 * i:256 * i + 256].bitcast(BF16) for i in range(3)]
            rden = xs[0][:, 1024:1536]
            otmp = [xs[0][:, 1536 + 512 * i:2048 + 512 * i] for i in range(2)]

            KT_b = [[Buf() for _ in range(16)] for _ in range(2)]
            VA_b = [Buf() for _ in range(64)]
            VAones_b = Buf()
            QT_b = [[Buf() for _ in range(16)] for _ in range(2)]
            Wb_b = Buf()
            xs_b = [Buf(), Buf()]
            xb_b = [Buf(), Buf()]
            sq_b = Buf()
            gq_b = [Buf() for _ in range(16)]
            gq0_b = Buf()
            biasK_b = [Buf() for _ in range(16)]
            lnt_b = Buf()
            rstd_b = [Buf(), Buf()]
            rcol_b = [Buf(), Buf()]
            fr_b, ee_b, t0_b, hib_b, r1_b, midb_b, r2_b = (Buf() for _ in range(7))
            ee_b = lnt_b
            cn_b = [Buf(), Buf()]
            PT_b = [Buf() for _ in range(3)]
            rden_b = Buf()
            otmp_b = [Buf(), Buf()]
            PS_b = [Buf() for _ in range(8)]

            Wb3 = Wb.rearrange("p (f n) -> p f n", n=776)
            WF3 = WF.rearrange("p (f n) -> p f n", n=128)
            VA4 = VA.rearrange("p (k q n) -> p k q n", q=2, n=192)
            KT3 = KT.rearrange("p (q t) -> p q t", t=8192)
            QT3 = QT.rearrange("p (q t) -> p q t", t=2048)
            ybT3 = ybT.rearrange("p (q t) -> p q t", t=2048)
            QT0_b = Buf()
            B.memset("pool", QT, 0.0, writes=[QT0_b])

            if p == 0:
                B.memset("pool", WF, 0.0, writes=[Wb_b])
            B.memset("pool", gq, 0.0, writes=[gq0_b])
            B.memset("pool", VA4[:, :, :, 64:128], 1.0, writes=[VAones_b])
            def pass_weight_steps(pp, slots, slot_bs, keys):
                col0 = [256 * pp, 512 + 256 * pp, 1024 + 256 * pp]
                steps = []
                for i in range(3):
                    def st(i=i):
                        sl = i % 2
                        st3 = slots[sl].rearrange("p (f n) -> p f n", n=256)
                        B.dma(st3, w_in_r[:, :, col0[i]:col0[i] + 256], keys[sl], writes=[slot_bs[sl]])
                        for fc in range(8):
                            B.ts("dve", Wb3[:, fc, 256 * i:256 * i + 256], st3[:, fc, :], gains[:, fc:fc + 1], ALU.mult,
                                 reads=[slot_bs[sl], prm_b], writes=[Wb_b])
                    steps.append(st)

                def stf_():
                    stf = slots[1][:, 0:32].rearrange("p (f n) -> p f n", n=4)
                    B.dma(stf, w_in_r[:, :, 1536 + 4 * pp:1536 + 4 * pp + 4], keys[1], writes=[slot_bs[1]])
                    for fc in range(8):
                        for g in range(3):
                            B.ts("dve", WF3[:, fc, 32 * g:32 * g + 4], stf[:, fc, :], gains[:, fc:fc + 1], ALU.mult,
                                 reads=[slot_bs[1], prm_b], writes=[Wb_b])
                steps.append(stf_)
                return steps

            pre_x = set()
            if p == 0:
                for ch0 in range(2):
                    B.dma(xs[ch0], xc[ch0], "ld_xs%d" % ch0, writes=[xs_b[ch0]])
                    pre_x.add(ch0)
                for st_ in pass_weight_steps(0, [sq.bitcast(F32), xb[1].bitcast(F32)], [sq_b, xb_b[1]], ["ld_w0", "ld_w1"]):
                    st_()

            def stageA1(ch):
                sl = ch % 2
                if ch not in pre_x:
                    B.dma(xs[sl], xc[ch], "ld_xs%d" % sl, writes=[xs_b[sl]])
                B.act(sq, xs[sl], AF.Square, reads=[xs_b[sl]], writes=[sq_b])
                B.cp("dve", xb[sl][:, 0:2048], xs[sl][:, 0:2048], reads=[xs_b[sl]], writes=[xb_b[sl]])
                B.act(xb[sl][:, 2048:4096], xs[sl][:, 2048:4096], AF.Copy, reads=[xs_b[sl]], writes=[xb_b[sl]])

            def stageA1mm(ch):
                sq3 = sq.rearrange("p (f t) -> p f t", t=512)
                for fc in range(8):
                    B.mm(PS[0][:, :], onesS, sq3[:, fc, :], fc == 0, fc == 7, reads=[sq_b, cbuf], writes=[PS_b[0]],
                         sig=(fc == 7))

            def stageA2act(ch):
                sl = ch % 2
                rstd_from_ms(PS[0], PS_b[0], lnt, lnt_b, rstd[sl], rstd_b[sl], 512)

            def stageA2rest(ch):
                sl = ch % 2
                rs = rstd[sl]
                for blk in range(4):
                    B.mm(PS[6][:, sl * 4 + blk:sl * 4 + blk + 1], rs[0:1, blk * 128:(blk + 1) * 128], onesf[0:1, 0:1], True, True,
                         reads=[rstd_b[sl], cbuf], writes=[PS_b[6]], sig=(blk == 3))
                B.cp("dve", rcol[sl], PS[6][:, sl * 4:sl * 4 + 4], reads=[PS_b[6]], writes=[rcol_b[sl]])

            def stageC(ch, do_pe, do_dve):
                cs = ch % 2
                if do_pe:
                    for blk in range(4):
                        B.mm(PS[6][:, 16 + cs * 16 + blk * 4:16 + cs * 16 + blk * 4 + 4], cn[cs][0:4, blk * 128:(blk + 1) * 128], identf[0:4, 0:4],
                             True, True, reads=[cn_b[cs], cbuf], writes=[PS_b[6]], sig=(blk == 3))
                if not do_dve:
                    return
                B.tt("dve", biasK[:, ch * 16:(ch + 1) * 16], PS[6][:, 16 + cs * 16:16 + cs * 16 + 16], kvalid[:, ch * 16:(ch + 1) * 16], ALU.add,
                     reads=[PS_b[6], cbuf], writes=[biasK_b[ch]])
                B.ts("dve", t0[0:72, :], cn[cs][0:72, 384:512], -8.0, ALU.mult, reads=[cn_b[cs]], writes=[t0_b])
                B.cp("dve", hib[0:72, :], t0[0:72, :], reads=[t0_b], writes=[hib_b])
                B.tt("dve", r1[0:72, :], t0[0:72, :], hib[0:72, :], ALU.subtract, reads=[t0_b, hib_b], writes=[r1_b])
                B.cp("dve", midb[0:72, :], r1[0:72, :], reads=[r1_b], writes=[midb_b])
                B.tt("dve", r2[0:72, :], r1[0:72, :], midb[0:72, :], ALU.subtract, reads=[r1_b, midb_b], writes=[r2_b])
                B.cp("dve", gq[0:4, ch * 128:(ch + 1) * 128], hib[0:4, :], reads=[hib_b, gq0_b], writes=[gq_b[ch]])
                B.cp("dve", gq[32:36, ch * 128:(ch + 1) * 128], midb[32:36, :], reads=[midb_b, gq0_b], writes=[gq_b[ch]])
                B.cp("dve", gq[64:68, ch * 128:(ch + 1) * 128], r2[64:68, :], reads=[r2_b, gq0_b], writes=[gq_b[ch]])

            def stageB1(ch):
                sl = ch % 2
                rs = rstd[sl]
                xb3 = xb[sl].rearrange("p (f t) -> p f t", t=512)
                for fc in range(8):
                    B.mm(PS[5][:, :], WF3[:, fc, :], xb3[:, fc, :], fc == 0, fc == 7,
                         reads=[Wb_b, xb_b[sl]], writes=[PS_b[5]], sig=(fc == 7))
                B.tt("dve", fr[0:72, :], PS[5][0:72, :], rs[0:72, :], ALU.mult, reads=[PS_b[5], rstd_b[sl]], writes=[fr_b])
                B.act(ee[0:72, :], fr[0:72, :], AF.Exp, bias=negbf[0:72, p:p + 1], scale=-1.0, reads=[fr_b, cbuf], writes=[ee_b])
                B.act(fr[0:72, :], ee[0:72, :], AF.Ln, bias=onesf[0:72, 0:1], scale=1.0, reads=[ee_b, cbuf], writes=[fr_b])
                cs = ch % 2
                init = 0.0 if ch == 0 else cn[1 - cs][0:72, 511:512]
                B.emit("dve", lambda E, o=cn[cs][0:72, :], d=fr[0:72, :], i=init: E.tensor_tensor_scan(
                    out=o, data0=d, data1=d, initial=i, op0=ALU.add, op1=ALU.bypass),
                    reads=[fr_b, cn_b[1 - cs]], writes=[cn_b[cs]])
                for pr in range(2):
                    bank = 1 + pr
                    for fc in range(8):
                        B.mm(PS[bank][:, :], Wb3[:, fc, 256 + pr * 128:256 + pr * 128 + 128], xb3[:, fc, :], fc == 0, fc == 7,
                             reads=[Wb_b, xb_b[sl]], writes=[PS_b[bank]], sig=(fc == 7))
                    B.tt("dve", KT3[:, pr, ch * 512:(ch + 1) * 512], PS[bank][:, :], rs, ALU.mult,
                         reads=[PS_b[bank], rstd_b[sl]], writes=[KT_b[pr][ch]])

            def stageB2(ch):
                sl = ch % 2
                xb3 = xb[sl].rearrange("p (f t) -> p f t", t=512)
                for blk in range(4):
                    bank = 3 + (blk % 2)
                    kb = ch * 4 + blk
                    for fc in range(8):
                        B.mm(PS[bank][:, 0:256], xb3[:, fc, blk * 128:(blk + 1) * 128], Wb3[:, fc, 512:768], fc == 0, fc == 7,
                             reads=[Wb_b, xb_b[sl]], writes=[PS_b[bank]], sig=(fc == 7))
                    vout = VA4[:, kb, :, :].rearrange("p q (g d) -> p q g d", d=64)[:, :, 0:3:2, :]
                    vin = PS[bank][:, 0:256].rearrange("p (q g d) -> p q g d", g=2, d=64)
                    B.act(vout, vin, AF.Copy, scale=rcol[sl][:, blk:blk + 1],
                          reads=[PS_b[bank], rcol_b[sl]], writes=[VA_b[kb]])

            def stageB3(ch):
                sl = ch % 2
                rs = rstd[sl]
                xb3 = xb[sl].rearrange("p (f t) -> p f t", t=512)
                if ch > 0:
                    stageC(ch - 1, True, False)
                for pr in range(2):
                    for fc in range(8):
                        B.mm(PS[7][:, pr * 128:(pr + 1) * 128], Wb3[:, fc, pr * 128:(pr + 1) * 128], xb3[:, fc, 384:512], fc == 0, fc == 7,
                             reads=[Wb_b, xb_b[sl]], writes=[PS_b[7]], sig=(fc == 7 and pr == 1))
                for pr in range(2):
                    for par in range(2):
                        rr = slice(par * 64, par * 64 + 64)
                        B.tt("dve", QT3[rr, 2 * pr + par, ch * 128:(ch + 1) * 128], PS[7][rr, pr * 128:(pr + 1) * 128], rs[rr, 384:512], ALU.mult,
                             reads=[PS_b[7], rstd_b[sl], QT0_b], writes=[QT_b[pr][ch]])
                if ch > 0:
                    stageC(ch - 1, False, True)

            stageA1(0)
            stageA1mm(0)
            stageA2act(0)
            stageA2rest(0)
            for ch in range(16):
                if ch + 1 < 16:
                    stageA1(ch + 1)
                stageB1(ch)
                if ch + 1 < 16:
                    stageA1mm(ch + 1)
                    stageA2act(ch + 1)
                stageB2(ch)
                if ch + 1 < 16:
                    stageA2rest(ch + 1)
                stageB3(ch)
            stageC(15, True, True)

            def inherit(dst, *srcs):
                for sb_ in srcs:
                    evs_ = list(sb_.r.items()) + ([sb_.w] if sb_.w is not None else [])
                    for k_, n_ in evs_:
                        if dst.r.get(k_, 0) < n_:
                            dst.r[k_] = n_
            for b_ in PT_b + [rden_b] + otmp_b:
                inherit(b_, xs_b[0])
            tiles = []
            for hl in range(4):
                for u in range(4):
                    nk = 16 * u + 16
                    for kb in range(nk):
                        tiles.append((hl, u, kb, kb == nk - 1))
            NT = len(tiles)
            LA = 2
            sev = [None] * NT
            deferred = {}

            def emit_S(i):
                hl, u, kb, last = tiles[i]
                pr, par = hl // 2, hl % 2
                rows = slice(par * 64, par * 64 + 64)
                sb = i % 3
                bank = PS[sb]
                kbl = kb - 16 * u
                mmin = 0 if kbl < 0 else kbl // 4
                c0 = 128 * mmin
                diag = kbl >= 0 and kbl % 4 == 3
                ch = kb // 4
                rd = [KT_b[pr][ch]] + [QT_b[pr][4 * u + m] for m in range(4)] + [gq_b[4 * u + m] for m in range(4)] + [cbuf, gq0_b, QT0_b]
                q0 = u * 512
                lk = KT3[:, pr, kb * 128:(kb + 1) * 128]

                def grp(ca, cb, mask):
                    B.mm(bank[:, ca:cb], lk, QT3[:, hl, q0 + ca:q0 + cb], True, False, reads=rd, writes=[PS_b[sb]])
                    ev = B.mm(bank[:, ca:cb], selb[:, hl * 128:(hl + 1) * 128], gq[:, q0 + ca:q0 + cb], False, not mask,
                              reads=rd, writes=[PS_b[sb]], sig=not mask)
                    if mask:
                        ev = B.mm(bank[:, ca:cb], identb, trib, False, True, reads=rd, writes=[PS_b[sb]], sig=True)
                    return ev
                if not diag:
                    grp(c0, 512, False)
                else:
                    if c0 + 128 < 512:
                        grp(c0 + 128, 512, False)
                    grp(c0, c0 + 128, True)

            def emit_EP(i):
                hl, u, kb, last = tiles[i]
                pr, par = hl // 2, hl % 2
                sb = i % 3
                kbl = kb - 16 * u
                mmin = 0 if kbl < 0 else kbl // 4
                c0 = 128 * mmin
                diag = kbl >= 0 and kbl % 4 == 3
                B.act(PT[sb][:, c0:512], PS[sb][:, c0:512], AF.Exp, bias=biasK[:, kb * 4 + hl:kb * 4 + hl + 1], scale=0.125,
                      reads=[PS_b[sb], biasK_b[kb // 4]], writes=[PT_b[sb]])
                ob = 3 + ((hl * 4 + u) % 2)
                lv = VA4[:, kb, pr, par * 64:par * 64 + 128]
                rdv = [VA_b[kb], VAones_b, PT_b[sb]]
                if not diag:
                    B.mm(PS[ob][:, c0:512], lv, PT[sb][:, c0:512], kb == 0, False, reads=rdv, writes=[PS_b[ob]], sig=True)
                else:
                    if c0 + 128 < 512:
                        B.mm(PS[ob][:, c0 + 128:512], lv, PT[sb][:, c0 + 128:512], kb == 0, False, reads=rdv, writes=[PS_b[ob]], sig=False)
                    B.mm(PS[ob][:, c0:c0 + 128], lv, PT[sb][:, c0:c0 + 128], kb == 0, last, reads=rdv, writes=[PS_b[ob]], sig=True)
                if last:
                    rows = slice(par * 64, par * 64 + 64)
                    r = 64 if par == 0 else 0
                    ot = otmp[(hl * 4 + u) % 2]
                    ot_b = otmp_b[(hl * 4 + u) % 2]
                    B.emit("dve", lambda E, o=rden[r:r + 1, :], s=PS[ob][r:r + 1, :]: E.reciprocal(out=o, in_=s),
                           reads=[PS_b[ob]], writes=[rden_b])
                    B.cp("dve", ot[rows, :], PS[ob][rows, :], reads=[PS_b[ob]], writes=[ot_b])
                    pg = 2 * p + pr

                    def fin(r=r, rows=rows, ot=ot, ot_b=ot_b, pg=pg, u=u):
                        B.mm(PS[5][:, :], onesf[r:r + 1, 0:128], rden[r:r + 1, :], True, True,
                             reads=[rden_b, cbuf], writes=[PS_b[5]], sig=True)
                        B.tt("dve", ybT3[rows, pg, u * 512:(u + 1) * 512], ot[rows, :], PS[5][rows, :], ALU.mult,
                             reads=[ot_b, PS_b[5]], writes=[ybT_b[pg * 4 + u]])
                    deferred[min(i + 2, NT - 1)] = fin

            if p == 0:
                pf_b = [Buf(), Buf()]
                inherit(pf_b[0], xs_b[1])
                inherit(pf_b[1], xs_b[1])
                pend.extend(pass_weight_steps(1, [xs[1][:, 0:2048], xs[1][:, 2048:4096]], pf_b, ["ld_pf0", "ld_pf1"]))
            if p == 1:
                R0 = XS_OFF + 16384
                WdA = V(R0, BF16, 8 * 1536)
                stg_all = V(R0 + 24576, F32, 4096)
                stg = [stg_all[:, 0:2048], stg_all[:, 2048:4096]]
                Wd1 = WdA.rearrange("p (f n) -> p f n", n=1536)
                for b_ in A1_bs:
                    inherit(b_, xs_b[1], xb_b[0])
                inherit(stg_b[0], xb_b[1])
                inherit(stg_b[1], sq_b)
                pend.extend(load_w_steps(Wd1, A1_bs, w_in_r[:, :, 1544:1544 + 1536], 1536, 0, dve_only=True))
            for i in range(min(LA, NT)):
                emit_S(i)
            for i in range(NT):
                if i + LA < NT:
                    emit_S(i + LA)
                emit_EP(i)
                if i in deferred:
                    deferred.pop(i)()
                if i % 64 == 40:
                    pump(1)
            assert not deferred
            flush()
            if p == 1:
                B.dma(stg_all, xo[0], "ld_st0", writes=stg_b)
            B.barrier()

        if DEBUG:
            a = Alloc(BASE, TOP)
            dtmp = a(F32, 4 * 2048)
            db = Buf()
            B.cp("dve", dtmp, ybT, writes=[db])
            B.dma(dbg_yb[:, :], dtmp, "st_dbg", reads=[db])
            B.barrier()

        a = Alloc(BASE, R0)
        t1 = a(BF16, 8 * 2048)
        WdB = a(BF16, 8 * 1536)
        D4_BASE = a.p
        hown = a(BF16, 8 * 2048)
        ya = a(BF16, 4 * 2048)
        sq = a(BF16, 4096)
        lnt = a(F32, 512)
        rstd1 = a(F32, 512)
        hh_ = a(BF16, 8 * 32)
        zh = a(F32, 4 * 32)
        zt = [a(F32, 4 * 130) for _ in range(2)]
        csb = [a(F32, 512) for _ in range(2)]
        a = Alloc(R0 + 40960, TOP)
        cv = [a(F32, 512) for _ in range(2)]
        sg = [a(F32, 512) for _ in range(2)]
        tm = [a(F32, 512) for _ in range(2)]

        t1_b = [[Buf() for _ in range(4)] for _ in range(8)]
        hown_b = [Buf() for _ in range(4)]
        ya_b = [[Buf() for _ in range(4)] for _ in range(4)]
        sq_b, lnt_b, rstd1_b, hh_b, zh_b = (Buf() for _ in range(5))
        zt_b = [Buf(), Buf()]
        csb_b = [Buf(), Buf()]
        cv_b = [Buf(), Buf()]
        sg_b = [Buf(), Buf()]
        tm_b = [Buf(), Buf()]
        PS_b = [Buf() for _ in range(8)]
        Abr_bs = [Buf() for _ in range(4)]
        Ag_bs = [Buf() for _ in range(8)]
        Bbr_bs = [Buf() for _ in range(4)]
        Bg_bs = [Buf() for _ in range(8)]
        Wo_bs = [Buf() for _ in range(8)]

        t13 = t1.rearrange("p (f t) -> p f t", t=2048)
        hown3 = hown.rearrange("p (f t) -> p f t", t=2048)
        ya3 = ya.rearrange("p (f t) -> p f t", t=2048)
        ybT3 = ybT.rearrange("p (q t) -> p q t", t=2048)
        sq3 = sq.rearrange("p (f t) -> p f t", t=512)

        def norm_stats(src3, src_rd, ncols):
            sqv = sq[:, 0:8 * ncols].rearrange("p (f t) -> p f t", t=ncols)
            B.act(sqv, src3, AF.Square, reads=src_rd, writes=[sq_b])
            for fc in range(8):
                B.mm(PS[0][:, 0:ncols], onesS, sqv[:, fc, :], fc == 0, fc == 7, reads=[sq_b, cbuf], writes=[PS_b[0]], sig=(fc == 7))
            rstd_from_ms(PS[0], PS_b[0], lnt, lnt_b, rstd1, rstd1_b, ncols)

        xsB = WdB[:, 0:8192].bitcast(F32)
        xsB_b = Buf()
        xhs = WdB[:, 8192:8192 + 512].bitcast(F32)
        xhs_b = Buf()
        B.dma(xhs, xh[:, :], "ld_xh", writes=[xhs_b])
        B.dma(xsB, xo[1], "ld_xsB", writes=[xsB_b])
        hh3 = hh_.rearrange("p (f t) -> p f t", t=32)
        def prelude_steps(tc):
            if tc % 2 == 0:
                xst, xst_bs, key = stg_all, stg_b, "ld_st0"
            else:
                xst, xst_bs, key = xsB, [xsB_b], "ld_xsB"
            xs03 = xst.rearrange("p (f t) -> p f t", t=512)

            def sA():
                if tc >= 2:
                    B.dma(xst, xo[tc], key, writes=xst_bs)
                norm_stats(xs03, xst_bs, 512)

            def mk(f0):
                def sB():
                    for fc in range(f0, f0 + 4):
                        B.tt("dve", hown3[:, fc, tc * 512:(tc + 1) * 512], xs03[:, fc, :], rstd1, ALU.mult,
                             reads=xst_bs + [rstd1_b], writes=[hown_b[tc]])
                return sB
            return [sA, mk(0), mk(4)]

        for st_ in prelude_steps(0):
            st_()
        xh3 = xhs.rearrange("p (f t) -> p f t", t=32)
        norm_stats(xh3, [xhs_b], 32)
        for fc in range(8):
            B.tt("dve", hh3[:, fc, :], xh3[:, fc, :], rstd1[:, 0:32], ALU.mult, reads=[xhs_b, rstd1_b], writes=[hh_b])

        Wd1 = WdA.rearrange("p (f n) -> p f n", n=1536)
        Abr3 = WdA[:, 0:4096].rearrange("p (f n) -> p f n", n=1024)
        Ag3 = WdA[:, 4096:4096 + 8192].rearrange("p (f n) -> p f n", n=1024)
        Bbr3 = WdB[:, 0:4096].rearrange("p (f n) -> p f n", n=1024)
        Bg3 = WdB[:, 4096:4096 + 8192].rearrange("p (f n) -> p f n", n=1024)
        Wo3 = WdB[:, 0:8192].rearrange("p (f n) -> p f n", n=1024)
        w_br_r = [w_br[i].rearrange("(fc p) n -> p fc n", p=128) for i in range(2)]

        g2 = load_w_steps(Bbr3, Bbr_bs, w_br_r[0], 1024, None, extra_w=[xsB_b, xhs_b]) + \
            load_w_steps(Bg3, Bg_bs, w_in_r[:, :, 3080:3080 + 1024], 1024, 0, extra_w=[xsB_b, xhs_b])
        pend.extend(prelude_steps(1) + prelude_steps(2) + prelude_steps(3) + g2)

        zh3 = zh.rearrange("p (f t) -> p f t", t=32)
        for fcw in range(4):
            for which, bank in ((1, 1), (2, 2)):
                for fc in range(8):
                    B.mm(PS[bank][:, 0:32], Wd1[:, fc, which * 512 + fcw * 128:which * 512 + fcw * 128 + 128], hh3[:, fc, :], fc == 0, fc == 7,
                         reads=[A1_bs[fc], hh_b], writes=[PS_b[bank]], sig=(fc == 7))
            B.cp("dve", csb[0][:, 0:32], PS[1][:, 0:32], reads=[PS_b[1]], writes=[csb_b[0]])
            B.tt("dve", zh3[:, fcw, :], csb[0][:, 0:32], PS[2][:, 0:32], ALU.mult, reads=[csb_b[0], PS_b[2]], writes=[zh_b])
        it = 0
        for tc in range(4):
            for fcw in range(4):
                s2 = it % 2
                it += 1
                banks = (1 + 3 * s2, 2 + 3 * s2, 3 + 3 * s2)
                for which in range(3):
                    bank = banks[which]
                    for fc in range(8):
                        B.mm(PS[bank][:, :], Wd1[:, fc, which * 512 + fcw * 128:which * 512 + fcw * 128 + 128], hown3[:, fc, tc * 512:(tc + 1) * 512],
                             fc == 0, fc == 7, reads=[A1_bs[fc], hown_b[tc]], writes=[PS_b[bank]], sig=(fc == 7))
                z3 = zt[s2].rearrange("p (b t) -> p b t", t=130)
                B.act(csb[s2], PS[banks[1]][:, :], AF.Copy, reads=[PS_b[banks[1]]], writes=[csb_b[s2]])
                B.cp("dve", z3[:, :, 0:2], zh3[:, fcw, tc * 8:(tc + 1) * 8].rearrange("p (b t) -> p b t", t=2), reads=[zh_b], writes=[zt_b[s2]])
                B.tt("dve", z3[:, :, 2:130], csb[s2].rearrange("p (b t) -> p b t", t=128), PS[banks[2]][:, :].rearrange("p (b t) -> p b t", t=128),
                     ALU.mult, reads=[csb_b[s2], PS_b[banks[2]]], writes=[zt_b[s2]])
                cv3 = cv[s2].rearrange("p (b t) -> p b t", t=128)
                B.ts("dve", cv3, z3[:, :, 0:128], convw[:, fcw * 3:fcw * 3 + 1], ALU.mult, reads=[zt_b[s2], prm_b], writes=[cv_b[s2]])
                B.stt("dve", cv3, z3[:, :, 1:129], convw[:, fcw * 3 + 1:fcw * 3 + 2], cv3, ALU.mult, ALU.add,
                      reads=[zt_b[s2], prm_b, cv_b[s2]], writes=[cv_b[s2]])
                B.stt("dve", cv3, z3[:, :, 2:130], convw[:, fcw * 3 + 2:fcw * 3 + 3], cv3, ALU.mult, ALU.add,
                      reads=[zt_b[s2], prm_b, cv_b[s2]], writes=[cv_b[s2]])
                B.tt("dve", ya3[:, fcw, tc * 512:(tc + 1) * 512], cv[s2], PS[banks[0]][:, :], ALU.mult,
                     reads=[cv_b[s2], PS_b[banks[0]]], writes=[ya_b[fcw][tc]])
                pump(2)
        flush()

        for br in range(2):
            if br == 0:
                Wbr3, Wbr_bs, Wg3, Wg_bs = Bbr3, Bbr_bs, Bg3, Bg_bs
                pend.extend(load_w_steps(Abr3, Abr_bs, w_br_r[1], 1024, None, extra_w=A1_bs))
                pend.extend(load_w_steps(Ag3, Ag_bs, w_in_r[:, :, 4104:4104 + 1024], 1024, 0, extra_w=A1_bs))
            else:
                Wbr3, Wbr_bs, Wg3, Wg_bs = Abr3, Abr_bs, Ag3, Ag_bs
                pend.extend(load_w_steps(Wo3, Wo_bs, w_out_r, 1024, None, extra_w=Bbr_bs + Bg_bs))
            it = 0
            for tc in range(4):
                for fo in range(8):
                    s2 = it % 2
                    it += 1
                    bg, bb = 1 + 2 * s2, 2 + 2 * s2
                    for fc in range(8):
                        B.mm(PS[bg][:, :], Wg3[:, fc, fo * 128:(fo + 1) * 128], hown3[:, fc, tc * 512:(tc + 1) * 512], fc == 0, fc == 7,
                             reads=[Wg_bs[fc], hown_b[tc]], writes=[PS_b[bg]], sig=(fc == 7))
                    for fcw in range(4):
                        if br == 0:
                            rhs, rb = ya3[:, fcw, tc * 512:(tc + 1) * 512], ya_b[fcw][tc]
                        else:
                            rhs, rb = ybT3[:, fcw, tc * 512:(tc + 1) * 512], ybT_b[fcw * 4 + tc]
                        B.mm(PS[bb][:, :], Wbr3[:, fcw, fo * 128:(fo + 1) * 128], rhs, fcw == 0, fcw == 3,
                             reads=[Wbr_bs[fcw], rb], writes=[PS_b[bb]], sig=(fcw == 3))
                    B.act(sg[s2], PS[bg][:, :], AF.Sigmoid, reads=[PS_b[bg]], writes=[sg_b[s2]])
                    if br == 0:
                        B.tt("dve", t13[:, fo, tc * 512:(tc + 1) * 512], sg[s2], PS[bb][:, :], ALU.mult,
                             reads=[sg_b[s2], PS_b[bb]], writes=[t1_b[fo][tc]])
                    else:
                        B.tt("dve", tm[s2], sg[s2], PS[bb][:, :], ALU.mult, reads=[sg_b[s2], PS_b[bb]], writes=[tm_b[s2]])
                        B.tt("dve", t13[:, fo, tc * 512:(tc + 1) * 512], tm[s2], t13[:, fo, tc * 512:(tc + 1) * 512], ALU.add,
                             reads=[tm_b[s2], t1_b[fo][tc]], writes=[t1_b[fo][tc]])
                    if it % 2 == 0:
                        pump(1)
            flush()
        allD = (hown_b + [b_ for l_ in ya_b for b_ in l_] + stg_b + [sq_b, lnt_b, rstd1_b, hh_b, zh_b, xsB_b, xhs_b]
                + zt_b + csb_b + cv_b + sg_b + tm_b + A1_bs + Abr_bs + Ag_bs + Bbr_bs + Bg_bs)
        PS_keep = PS_b

        X1_OFF = TOP - 8 * 2048 * 4
        x1 = V(X1_OFF, F32, 8 * 2048)
        x13 = x1.rearrange("p (f t) -> p f t", t=2048)
        x1_b = [Buf() for _ in range(4)]
        a = Alloc(D4_BASE, X1_OFF)
        msbs = [a(F32, 4096) for _ in range(2)]
        sqs = [a(BF16, 4096) for _ in range(2)]
        lnt = a(F32, 512)
        rstd1 = a(F32, 512)
        tm = [a(F32, 512) for _ in range(2)]
        msbs_b = [Buf(), Buf()]
        sqs_b = [Buf(), Buf()]
        lnt_b, rstd1_b = Buf(), Buf()
        tm_b = [Buf(), Buf()]
        PS_b = PS_keep
        for b_ in x1_b + msbs_b + sqs_b + [lnt_b, rstd1_b] + tm_b:
            inherit(b_, *allD)
        for tc in range(4):
            for fc in range(8):
                B.dma(x13[:, fc, tc * 512:(tc + 1) * 512], xo[tc][:, fc * 512:(fc + 1) * 512], "ld_x1_%d" % tc, writes=[x1_b[tc]])

        def post_norm_residual(tc_glob, fsb3, fsb_b, gain_off):
            for fc in range(8):
                B.mm(PS[0][:, :], onesS, sq3[:, fc, :], fc == 0, fc == 7, reads=[sq_b, cbuf], writes=[PS_b[0]], sig=(fc == 7))
            rstd_from_ms(PS[0], PS_b[0], lnt, lnt_b, rstd1, rstd1_b, 512)
            for fo in range(8):
                s2 = fo % 2
                B.tt("dve", tm[s2], fsb3[:, fo, :], rstd1, ALU.mult, reads=[fsb_b, rstd1_b], writes=[tm_b[s2]])
                xsl = x13[:, fo, tc_glob * 512:(tc_glob + 1) * 512]
                B.stt("dve", xsl, tm[s2], gains[:, gain_off + fo:gain_off + fo + 1], xsl, ALU.mult, ALU.add,
                      reads=[tm_b[s2], prm_b, x1_b[tc_glob]], writes=[x1_b[tc_glob]])

        it = 0
        for tc in range(4):
            msb3 = msbs[tc % 2].rearrange("p (f t) -> p f t", t=512)
            msb_b = msbs_b[tc % 2]
            sq3 = sqs[tc % 2].rearrange("p (f t) -> p f t", t=512)
            sq_b = sqs_b[tc % 2]
            for fo in range(8):
                bank = 1 + it % 4
                it += 1
                for fc in range(8):
                    B.mm(PS[bank][:, :], Wo3[:, fc, fo * 128:(fo + 1) * 128], t13[:, fc, tc * 512:(tc + 1) * 512], fc == 0, fc == 7,
                         reads=[Wo_bs[fc], t1_b[fc][tc]], writes=[PS_b[bank]], sig=(fc == 7))
                B.act(msb3[:, fo, :], PS[bank][:, :], AF.Copy, reads=[PS_b[bank]], writes=[msb_b])
                B.act(sq3[:, fo, :], PS[bank][:, :], AF.Square, reads=[PS_b[bank]], writes=[sq_b])
            post_norm_residual(tc, msb3, msb_b, 8)
        B.barrier()

        if DEBUG:
            B.dma(dbg_x1[:, :], x1, "st_dbg", reads=x1_b)
            B.barrier()

        out_evs = []
        for half in range(2):
            a = Alloc(BASE0, X1_OFF)
            aT = a(BF16, 22 * 1024)
            E1_BASE = a.p
            hfT = a(BF16, 8 * 1024)
            stg2 = [a(F32, 4096) for _ in range(2)]
            wgb = [a(BF16, 4096) for _ in range(2)]
            SQ_OFF = a.p
            sq = a(BF16, 4096)
            lnt = a(F32, 512)
            rstd1 = a(F32, 512)
            SG_OFF = a.p
            sg = [a(F32, 512) for _ in range(2)]
            FREE_OFF = a.p
            stg3 = V(SQ_OFF, F32, 11 * 256)
            wdb = [V(FREE_OFF, BF16, 22 * 256), None]
            stg3_b = Buf()
            wdb_b = [[Buf(), Buf()] for _ in range(2)]
            st3 = stg3.rearrange("p (h n) -> p h n", n=256)
            aT_b = [[Buf() for _ in range(2)] for _ in range(22)]
            hfT_b = [Buf(), Buf()]
            stg2_b = [Buf(), Buf()]
            wgb_b = [[Buf() for _ in range(8)] for _ in range(2)]
            sq_b, lnt_b, rstd1_b = Buf(), Buf(), Buf()
            sg_b = [Buf(), Buf()]
            PS_b = [Buf() for _ in range(8)]
            aT3 = aT.rearrange("p (h t) -> p h t", t=1024)
            hfT3 = hfT.rearrange("p (f t) -> p f t", t=1024)
            sq3 = sq.rearrange("p (f t) -> p f t", t=512)

            def gu_steps(hb):
                sl = hb % 2
                st4 = stg2[sl].rearrange("p (f g n) -> p f g n", g=2, n=256)

                def s0():
                    for g in range(2):
                        B.dma(st4[:, :, g, :], w_gu_r[:, :, g * FFN + hb * 256:g * FFN + hb * 256 + 256], "ld_sg%d" % sl, writes=[stg2_b[sl]])

                def mk(fc):
                    def s():
                        if fc % 2:
                            B.act(wgb[sl][:, fc * 512:(fc + 1) * 512], stg2[sl][:, fc * 512:(fc + 1) * 512], AF.Copy,
                                  scale=gains[:, 16 + fc:16 + fc + 1], reads=[stg2_b[sl], prm_b], writes=[wgb_b[sl][fc]])
                        else:
                            B.ts("dve", wgb[sl][:, fc * 512:(fc + 1) * 512], stg2[sl][:, fc * 512:(fc + 1) * 512],
                                 gains[:, 16 + fc:16 + fc + 1], ALU.mult, reads=[stg2_b[sl], prm_b], writes=[wgb_b[sl][fc]])
                    return s
                return [s0] + [mk(fc) for fc in range(8)]

            def dn_steps(cb):
                sl = cb % 2
                wd3 = wdb[sl].rearrange("p (h n) -> p h n", n=256)

                def mk(hh2):
                    def s():
                        ex = [e0_sq_b, e0_lnt_b, e0_rstd_b] if (cb == 0 and hh2 == 0) else []
                        B.dma(st3, w_dn_r[:, hh2 * 11:(hh2 + 1) * 11, cb * 256:(cb + 1) * 256], "ld_s3", writes=[stg3_b] + ex)
                        B.cp("dve", wd3[:, hh2 * 11:hh2 * 11 + 6, :], st3[:, 0:6, :], reads=[stg3_b], writes=[wdb_b[sl][hh2]])
                        B.act(wd3[:, hh2 * 11 + 6:hh2 * 11 + 11, :], st3[:, 6:11, :], AF.Copy, reads=[stg3_b], writes=[wdb_b[sl][hh2]])
                    return s
                return [mk(0), mk(1)]

            e0_sq_b, e0_lnt_b, e0_rstd_b = sq_b, lnt_b, rstd1_b
            pend.extend(gu_steps(0))
            pump(1)
            for tcc in range(2):
                tg = half * 2 + tcc
                B.act(sq3, x13[:, :, tg * 512:(tg + 1) * 512], AF.Square, reads=[x1_b[tg]], writes=[sq_b])
                for fc in range(8):
                    B.mm(PS[0][:, :], onesS, sq3[:, fc, :], fc == 0, fc == 7, reads=[sq_b, cbuf], writes=[PS_b[0]], sig=(fc == 7))
                rstd_from_ms(PS[0], PS_b[0], lnt, lnt_b, rstd1, rstd1_b, 512)
                if tcc == 0:
                    pump(8)
                for fc in range(8):
                    B.tt("dve", hfT3[:, fc, tcc * 512:(tcc + 1) * 512], x13[:, fc, tg * 512:(tg + 1) * 512], rstd1, ALU.mult,
                         reads=[x1_b[tg], rstd1_b], writes=[hfT_b[tcc]])
                pump(5)
            flush()
            it = 0
            for hb in range(11):
                sl = hb % 2
                wg4 = wgb[sl].rearrange("p (f g n) -> p f g n", g=2, n=256)
                if hb + 1 < 11:
                    pend.extend(gu_steps(hb + 1))
                    pump(1)
                if hb == 10:
                    pend.extend(dn_steps(0))
                for tcc in range(2):
                    for hc2 in range(2):
                        s2 = it % 2
                        it += 1
                        bg, bu = 1 + 2 * s2, 2 + 2 * s2
                        for g, bank in ((0, bg), (1, bu)):
                            for fc in range(8):
                                B.mm(PS[bank][:, :], wg4[:, fc, g, hc2 * 128:(hc2 + 1) * 128], hfT3[:, fc, tcc * 512:(tcc + 1) * 512], fc == 0, fc == 7,
                                     reads=[wgb_b[sl][fc], hfT_b[tcc]], writes=[PS_b[bank]], sig=(fc == 7))
                        B.act(sg[s2], PS[bg][:, :], AF.Silu, reads=[PS_b[bg]], writes=[sg_b[s2]])
                        B.tt("dve", aT3[:, hb * 2 + hc2, tcc * 512:(tcc + 1) * 512], sg[s2], PS[bu][:, :], ALU.mult,
                             reads=[sg_b[s2], PS_b[bu]], writes=[aT_b[hb * 2 + hc2][tcc]])
                        pump(2)
                flush()
            B.barrier()
            a = Alloc(E1_BASE, SQ_OFF)
            fsb = a(F32, 2 * 4096)
            sqd = a(BF16, 2 * 4096)
            wdb[1] = a(BF16, 22 * 256)
            lnt = a(F32, 512)
            rstd1 = a(F32, 512)
            tm = [V(SG_OFF, F32, 512), V(SG_OFF + 2048, F32, 512)]
            fsb_b = [Buf(), Buf()]
            sqd_b = [Buf(), Buf()]
            lnt_b, rstd1_b = Buf(), Buf()
            tm_b = [Buf(), Buf()]
            PS_b = [Buf() for _ in range(8)]
            fsb4 = fsb.rearrange("p (c f t) -> p c f t", f=8, t=512)
            sqd4 = sqd.rearrange("p (c f t) -> p c f t", f=8, t=512)
            it = 0
            for cb in range(4):
                sl = cb % 2
                wd3 = wdb[sl].rearrange("p (h n) -> p h n", n=256)
                if cb + 1 < 4:
                    pend.extend(dn_steps(cb + 1))
                for tcc in range(2):
                    for fo2 in range(2):
                        fo = cb * 2 + fo2
                        bank = 1 + it % 4
                        it += 1
                        for hc in range(22):
                            B.mm(PS[bank][:, :], wd3[:, hc, fo2 * 128:(fo2 + 1) * 128], aT3[:, hc, tcc * 512:(tcc + 1) * 512], hc == 0, hc == 21,
                                 reads=[wdb_b[sl][hc // 11], aT_b[hc][tcc]], writes=[PS_b[bank]], sig=(hc == 21))
                        B.act(fsb4[:, tcc, fo, :], PS[bank][:, :], AF.Copy, reads=[PS_b[bank]], writes=[fsb_b[tcc]])
                        B.act(sqd4[:, tcc, fo, :], PS[bank][:, :], AF.Square, reads=[PS_b[bank]], writes=[sqd_b[tcc]])
                        if fo2 == 0:
                            pump(1)
                    if cb == 3:
                        tg = half * 2 + tcc
                        sq3 = sqd4[:, tcc]
                        sq_b = sqd_b[tcc]
                        post_norm_residual(tg, fsb4[:, tcc], fsb_b[tcc], 24)
                        for fc in range(8):
                            out_evs.append(B.dma(y[tg][:, fc * 512:(fc + 1) * 512], x13[:, fc, tg * 512:(tg + 1) * 512], "st_y", reads=[x1_b[tg]]))
                flush()
            B.barrier(exclude=("st_y",))


        for ev in out_evs[-1:]:
            B._wait("sp", ev)
        B.ops["sp"].append(lambda E: E.wait_ge(B.sem["st_y"], B.cnt["st_y"]))
        if DEBUG:
            B.ops["sp"].append(lambda E: E.wait_ge(B.sem["st_dbg"], B.cnt["st_dbg"]))

        with nc.Block() as block:
            @block.tensor
            def _(E):
                for f in B.ops["pe"]:
                    f(E)

            @block.scalar
            def _(E):
                for f in B.ops["act"]:
                    f(E)

            @block.vector
            def _(E):
                for f in B.ops["dve"]:
                    f(E)

            @block.gpsimd
            def _(E):
                for f in B.ops["pool"]:
                    f(E)

            @block.sync
            def _(E):
                for f in B.ops["sp"]:
                    f(E)
    return nc


def _relayout_T(xt):
    n = xt.shape[0] // 512
    return np.ascontiguousarray(xt.reshape(n, 512, 8, 128).transpose(0, 3, 2, 1)).reshape(n, 128, 8 * 512)


def make_inputs(x, norm_mix_pre, norm_mix_post, w_in, b_f, conv_w, w_branch, w_out,
                norm_ffn_pre, norm_ffn_post, w_gate_up, w_down):
    f32 = np.float32
    x = np.asarray(x, f32)
    gains = np.stack([np.asarray(g, f32)[0] for g in (norm_mix_pre, norm_mix_post, norm_ffn_pre, norm_ffn_post)])
    gains_l = np.ascontiguousarray(gains.reshape(4, 8, 128).transpose(2, 0, 1)).reshape(128, 32)
    cw = np.asarray(conv_w, f32)[0]
    cw_l = np.ascontiguousarray(cw.reshape(3, 4, 128).transpose(2, 1, 0)).reshape(128, 12)
    bf = np.asarray(b_f, f32)[0]
    ident = np.eye(128, dtype=f32)
    kk = np.arange(128)[:, None]
    qq = np.arange(128)[None, :]
    tri = np.where(qq >= kk, 0.0, NEG).astype(f32)
    sel = np.zeros((128, 8, 128), f32)
    for h in range(8):
        for g in range(3):
            sel[32 * g + (h % 4), h] = 1.0
    sel = sel.reshape(128, 1024)
    w_in0 = np.ascontiguousarray(np.asarray(w_in, f32)[0])
    w_br0 = np.ascontiguousarray(np.asarray(w_branch, f32)[0])
    w_out0 = np.ascontiguousarray(np.asarray(w_out, f32)[0])
    w_gu0 = np.ascontiguousarray(np.asarray(w_gate_up, f32)[0])
    w_dn0 = np.ascontiguousarray(np.asarray(w_down, f32)[0])
    in_maps = []
    for c in range(8):
        b, j = c // 4, c % 4
        npad = (3 - j) * 128
        ctx = np.zeros((SEQ, D), f32)
        ctx[npad:] = x[b, :SEQ - npad]
        own_blocks = [ctx[(4 * s + 3) * 128:(4 * s + 4) * 128] for s in range(16)]
        own = np.concatenate(own_blocks, 0)
        halo = np.concatenate([ctx[(4 * s + 3) * 128 - 2:(4 * s + 3) * 128] for s in range(16)], 0)
        xh = np.ascontiguousarray(halo.reshape(32, 8, 128).transpose(2, 1, 0)).reshape(128, 256)
        prm = np.zeros((128, 48), f32)
        prm[:, 0:32] = gains_l
        prm[:, 32:44] = cw_l
        kval = np.zeros((128, 64, 4), f32)
        kval[:, :3 - j, :] = NEG
        cst = np.concatenate([ident, tri, sel, kval.reshape(128, 256)], 1)
        in_maps.append(dict(xc=_relayout_T(ctx), xo=_relayout_T(own), xh=xh, w_in=w_in0, w_br=w_br0, w_out=w_out0,
                            w_gu=w_gu0, w_dn=w_dn0, prm=prm, cst=cst, _bf=bf))
    return in_maps


_NC = None


def kernel(x, norm_mix_pre, norm_mix_post, w_in, b_f, conv_w, w_branch, w_out,
           norm_ffn_pre, norm_ffn_post, w_gate_up, w_down):
    global _NC
    in_maps = make_inputs(x, norm_mix_pre, norm_mix_post, w_in, b_f, conv_w, w_branch, w_out,
                          norm_ffn_pre, norm_ffn_post, w_gate_up, w_down)
    if _NC is None:
        _NC = build_program()
    for m in in_maps:
        bf = m.pop("_bf")
        for pss in range(2):
            for g in range(3):
                m["prm"][32 * g:32 * g + 4, 44 + pss] = bf[4 * pss:4 * pss + 4]
    res = run_bass_kernel_spmd(_NC, in_maps, core_ids=list(range(8)))
    out = np.zeros((NB, SEQ, D), np.float32)
    for c in range(8):
        b, j = c // 4, c % 4
        yc = res.results[c]["y"].reshape(4, 128, 8, 512)
        own = yc.transpose(0, 3, 2, 1).reshape(2048, D)
        for s in range(16):
            blk = 4 * s + j
            out[b, blk * 128:(blk + 1) * 128] = own[s * 128:(s + 1) * 128]
    kernel.last_results = res
    return out
```

```python
import numpy as np
from contextlib import ExitStack
import concourse.bass as bass
import concourse.mybir as mybir
from concourse.bass_utils import run_bass_kernel_spmd

F32 = mybir.dt.float32
BF16 = mybir.dt.bfloat16
AF = mybir.ActivationFunctionType
ALU = mybir.AluOpType

D = 1024
SEQ = 8192
NB = 2
FFN = 2816
NEG = -30000.0
ARENA_ELEMS = 106368
DEBUG = False

ENG = ("pe", "act", "dve", "pool", "sp")


class Buf:
    __slots__ = ("w", "r")

    def __init__(self):
        self.w = None
        self.r = {}


class Builder:
    def __init__(self, nc, stack):
        self.nc = nc
        self.stack = stack
        self.ops = {e: [] for e in ENG}
        self.cnt = {}
        self.sem = {}
        self.waited = {e: {} for e in ENG}
        self.phase = 0

    def getsem(self, key):
        if key not in self.sem:
            self.sem[key] = self.stack.enter_context(self.nc.semaphore(key))
            self.cnt[key] = 0
        return self.sem[key]

    def _wait(self, eng, ev):
        if ev is None:
            return
        key, n = ev
        if eng == "pe" and key.startswith("pe_"):
            return
        if self.waited[eng].get(key, 0) >= n:
            return
        self.waited[eng][key] = n
        sem = self.sem[key]
        self.ops[eng].append(lambda E: E.wait_ge(sem, n))

    def _hazards(self, eng, reads, writes, deps):
        for b in reads:
            self._wait(eng, b.w)
        for b in writes:
            self._wait(eng, b.w)
            for k, n in b.r.items():
                self._wait(eng, (k, n))
        for d in deps:
            self._wait(eng, d)

    def _record(self, ev, reads, writes):
        for b in reads:
            if b.r.get(ev[0], 0) < ev[1]:
                b.r[ev[0]] = ev[1]
        for b in writes:
            b.w = ev
            b.r = {}

    def emit(self, eng, fn, reads=(), writes=(), deps=(), sig=True):
        self._hazards(eng, reads, writes, deps)
        key = "%s_%d" % (eng, self.phase)
        sem = self.getsem(key)
        if sig:
            self.cnt[key] += 1
            n = self.cnt[key]
            self.ops[eng].append(lambda E: fn(E).then_inc(sem, 1))
        else:
            n = self.cnt[key] + 1
            self.ops[eng].append(lambda E: fn(E))
        ev = (key, n)
        self._record(ev, reads, writes)
        return ev

    def dma(self, out, in_, semkey, reads=(), writes=(), deps=(), q="sp"):
        self._hazards(q, reads, writes, deps)
        sem = self.getsem(semkey)
        self.cnt[semkey] += 16
        n = self.cnt[semkey]
        self.ops[q].append(lambda E: E.dma_start(out=out, in_=in_).then_inc(sem, 16))
        ev = (semkey, n)
        self._record(ev, reads, writes)
        return ev

    def barrier(self, exclude=()):
        evs = [(k, n) for k, n in self.cnt.items() if n > 0 and k not in exclude]
        for e in ENG:
            for ev in evs:
                self._wait(e, ev)
        self.phase += 1

    def mm(self, out, lhsT, rhs, start, stop, reads=(), writes=(), deps=(), sig=False):
        return self.emit("pe", lambda E: E.matmul(out, lhsT=lhsT, rhs=rhs, start=start, stop=stop),
                         reads, writes, deps, sig)

    def act(self, out, in_, func, bias=None, scale=None, reads=(), writes=(), deps=()):
        kw = {}
        if bias is not None:
            kw["bias"] = bias
        if scale is not None:
            kw["scale"] = scale
        return self.emit("act", lambda E: E.activation(out=out, in_=in_, func=func, **kw), reads, writes, deps)

    def tt(self, eng, out, in0, in1, op, reads=(), writes=(), deps=()):
        return self.emit(eng, lambda E: E.tensor_tensor(out=out, in0=in0, in1=in1, op=op), reads, writes, deps)

    def ts(self, eng, out, in0, s1, op0, s2=None, op1=None, reads=(), writes=(), deps=()):
        if op1 is None:
            return self.emit(eng, lambda E: E.tensor_scalar(out=out, in0=in0, scalar1=s1, scalar2=None, op0=op0),
                             reads, writes, deps)
        return self.emit(eng, lambda E: E.tensor_scalar(out=out, in0=in0, scalar1=s1, scalar2=s2, op0=op0, op1=op1),
                         reads, writes, deps)

    def stt(self, eng, out, in0, scalar, in1, op0, op1, reads=(), writes=(), deps=()):
        return self.emit(eng, lambda E: E.scalar_tensor_tensor(out=out, in0=in0, scalar=scalar, in1=in1, op0=op0, op1=op1),
                         reads, writes, deps)

    def cp(self, eng, out, in_, reads=(), writes=(), deps=()):
        return self.emit(eng, lambda E: E.tensor_copy(out=out, in_=in_), reads, writes, deps)

    def memset(self, eng, ap, val, writes=()):
        return self.emit(eng, lambda E: E.memset(ap, val), (), writes, ())


def build_program():
    nc = bass.Bass("TRN2", target_bir_lowering=False)
    dt = nc.dram_tensor
    xc = dt("xc", [16, 128, 8 * 512], F32, kind="ExternalInput").ap()
    xo = dt("xo", [4, 128, 8 * 512], F32, kind="ExternalInput").ap()
    xh = dt("xh", [128, 8 * 32], F32, kind="ExternalInput").ap()
    w_in = dt("w_in", [D, 5128], F32, kind="ExternalInput").ap()
    w_br = dt("w_br", [2, 512, D], F32, kind="ExternalInput").ap()
    w_out = dt("w_out", [D, D], F32, kind="ExternalInput").ap()
    w_gu = dt("w_gu", [D, 2 * FFN], F32, kind="ExternalInput").ap()
    w_dn = dt("w_dn", [FFN, D], F32, kind="ExternalInput").ap()
    prm = dt("prm", [128, 48], F32, kind="ExternalInput").ap()
    NCST = 128 + 128 + 1024 + 256
    cst = dt("cst", [128, NCST], F32, kind="ExternalInput").ap()
    y = dt("y", [4, 128, 8 * 512], F32, kind="ExternalOutput").ap()
    if DEBUG:
        dbg_yb = dt("dbg_yb", [128, 4 * 2048], F32, kind="ExternalOutput").ap()
        dbg_x1 = dt("dbg_x1", [128, 8 * 2048], F32, kind="ExternalOutput").ap()

    w_in_r = w_in.rearrange("(fc p) n -> p fc n", p=128)
    w_gu_r = w_gu.rearrange("(fc p) n -> p fc n", p=128)
    w_out_r = w_out.rearrange("(fc p) n -> p fc n", p=128)
    w_dn_r = w_dn.rearrange("(hc p) n -> p hc n", p=128)

    with ExitStack() as stack:
        arena = stack.enter_context(nc.sbuf_tensor("arena", [128, ARENA_ELEMS], BF16))
        PS = [stack.enter_context(nc.psum_tensor("ps%d" % i, [128, 512], F32)) for i in range(8)]
        bld = Builder(nc, stack)
        B = bld

        def V(off, dtype, cols):
            assert off % 4 == 0
            if dtype == BF16:
                assert off // 2 + cols <= ARENA_ELEMS, (off, cols)
                return arena[:, off // 2: off // 2 + cols]
            assert off // 2 + 2 * cols <= ARENA_ELEMS, (off, cols)
            return arena[:, off // 2: off // 2 + 2 * cols].bitcast(F32)

        class Alloc:
            def __init__(self, base, limit):
                self.p = base
                self.limit = limit

            def __call__(self, dtype, cols):
                nb = cols * (2 if dtype == BF16 else 4)
                nb = (nb + 63) // 64 * 64
                off = self.p
                self.p += nb
                assert self.p <= self.limit, ("arena overflow", self.p, self.limit)
                return V(off, dtype, cols)

        TOP = ARENA_ELEMS * 2
        al = Alloc(0, TOP)
        prm_sb = al(F32, 48)
        prm_b = Buf()
        gains = prm_sb[:, 0:32]
        convw = prm_sb[:, 32:44]
        bfcol = prm_sb[:, 44:46]
        negbf = al(F32, 2)
        cst_f = al(F32, 128 + 256)
        identf = cst_f[:, 0:128]
        kvalid = cst_f[:, 128:384]
        onesf = al(F32, 128)
        identb = al(BF16, 128)
        trib = al(BF16, 128)
        selb = al(BF16, 1024)
        onesS = al(BF16, 128)
        cbuf = Buf()
        epsc = al(F32, 1)
        BASE0 = al.p
        ybT = al(BF16, 4 * 2048)
        ybT_b = [Buf() for _ in range(16)]
        BASE = al.p

        tmp = Alloc(BASE, TOP)
        cst_st = tmp(F32, NCST)
        st_b = Buf()
        B.dma(prm_sb, prm[:, :], "ld_c", writes=[prm_b])
        B.dma(cst_st, cst[:, :], "ld_c2", writes=[st_b])
        B.ts("dve", negbf, bfcol, -1.0, ALU.mult, reads=[prm_b], writes=[cbuf])
        B.cp("dve", cst_f[:, 0:128], cst_st[:, 0:128], reads=[st_b], writes=[cbuf])
        B.cp("dve", kvalid, cst_st[:, 1280:1536], reads=[st_b], writes=[cbuf])
        B.cp("dve", identb, cst_st[:, 0:128], reads=[st_b], writes=[cbuf])
        B.cp("dve", trib, cst_st[:, 128:256], reads=[st_b], writes=[cbuf])
        B.cp("dve", selb, cst_st[:, 256:1280], reads=[st_b], writes=[cbuf])
        B.memset("dve", onesf, 1.0, writes=[cbuf])
        B.memset("dve", onesS, 1.0 / 1024.0, writes=[cbuf])
        B.memset("dve", epsc, 1e-6, writes=[cbuf])
        B.barrier()

        EPS = 1e-6

        def rstd_from_ms(ms_ps, ms_b, lnt, lnt_b, rstd, rstd_b, ncols):
            B.act(lnt[:, 0:ncols], ms_ps[:, 0:ncols], AF.Ln, bias=epsc, reads=[ms_b], writes=[lnt_b])
            B.act(rstd[:, 0:ncols], lnt[:, 0:ncols], AF.Exp, scale=-0.5, reads=[lnt_b], writes=[rstd_b])


        pend = []

        def pump(n=1):
            for _ in range(n):
                if pend:
                    pend.pop(0)()

        def flush():
            while pend:
                pend.pop(0)()

        stg_b = [Buf(), Buf()]
        A1_bs = [Buf() for _ in range(8)]
        lw_k = [0]

        def load_w_steps(dst3, dst_bs, src3, ncols, gain_off, extra_w=(), dve_only=False):
            nfc = dst3.shape[1]
            steps = []
            for c in range(0, ncols, 256):
                w = min(256, ncols - c)

                def step(c=c, w=w):
                    sl = lw_k[0] % 2
                    lw_k[0] += 1
                    st3 = stg[sl][:, 0:nfc * w].rearrange("p (f n) -> p f n", n=w)
                    B.dma(st3, src3[:, :, c:c + w], "ld_st%d" % sl, writes=[stg_b[sl]])
                    for f in range(nfc):
                        wr = [dst_bs[f]] + list(extra_w)
                        if gain_off is None:
                            if f % 2 and not dve_only:
                                B.act(dst3[:, f, c:c + w], st3[:, f, :], AF.Copy, reads=[stg_b[sl]], writes=wr)
                            else:
                                B.cp("dve", dst3[:, f, c:c + w], st3[:, f, :], reads=[stg_b[sl]], writes=wr)
                        else:
                            gc = gains[:, gain_off + f:gain_off + f + 1]
                            if f % 2 and not dve_only:
                                B.act(dst3[:, f, c:c + w], st3[:, f, :], AF.Copy, scale=gc, reads=[stg_b[sl], prm_b], writes=wr)
                            else:
                                B.ts("dve", dst3[:, f, c:c + w], st3[:, f, :], gc, ALU.mult, reads=[stg_b[sl], prm_b], writes=wr)
                steps.append(step)
            return steps

        for p in range(2):
            a = Alloc(BASE, TOP)
            KT = a(BF16, 2 * 8192)
            VA = a(BF16, 64 * 2 * 192)
            QT = a(BF16, 4 * 2048)
            Wb = a(BF16, 8 * 776)
            WF = a(BF16, 8 * 128)
            XS_OFF = a.p
            xs = [a(F32, 4096) for _ in range(2)]
            xb = [a(BF16, 4096) for _ in range(2)]
            sq = a(BF16, 4096)
            gq = a(BF16, 2048)
            biasK = a(F32, 256)
            lnt = a(F32, 512)
            rstd = [a(F32, 512) for _ in range(2)]
            rcol = [a(F32, 4) for _ in range(2)]
            fr = a(F32, 512)
            ee = lnt
            cn = [a(F32, 512) for _ in range(2)]
            t0 = a(F32, 128)
            hib = a(BF16, 128)
            r1 = a(F32, 128)
            midb = a(BF16, 128)
            r2 = a(F32, 128)
            PT = [xs[0][:, 256 * i:256 * i + 256].bitcast(BF16) for i in range(3)]
            rden = xs[0][:, 1024:1536]
            otmp = [xs[0][:, 1536 + 512 * i:2048 + 512 * i] for i in range(2)]

            KT_b = [[Buf() for _ in range(16)] for _ in range(2)]
            VA_b = [Buf() for _ in range(64)]
            VAones_b = Buf()
            QT_b = [[Buf() for _ in range(16)] for _ in range(2)]
            Wb_b = Buf()
            xs_b = [Buf(), Buf()]
            xb_b = [Buf(), Buf()]
            sq_b = Buf()
            gq_b = [Buf() for _ in range(16)]
            gq0_b = Buf()
            biasK_b = [Buf() for _ in range(16)]
            lnt_b = Buf()
            rstd_b = [Buf(), Buf()]
            rcol_b = [Buf(), Buf()]
            fr_b, ee_b, t0_b, hib_b, r1_b, midb_b, r2_b = (Buf() for _ in range(7))
            ee_b = lnt_b
            cn_b = [Buf(), Buf()]
            PT_b = [Buf() for _ in range(3)]
            rden_b = Buf()
            otmp_b = [Buf(), Buf()]
            PS_b = [Buf() for _ in range(8)]

            Wb3 = Wb.rearrange("p (f n) -> p f n", n=776)
            WF3 = WF.rearrange("p (f n) -> p f n", n=128)
            VA4 = VA.rearrange("p (k q n) -> p k q n", q=2, n=192)
            KT3 = KT.rearrange("p (q t) -> p q t", t=8192)
            QT3 = QT.rearrange("p (q t) -> p q t", t=2048)
            ybT3 = ybT.rearrange("p (q t) -> p q t", t=2048)
            QT0_b = Buf()
            B.memset("pool", QT, 0.0, writes=[QT0_b])

            if p == 0:
                B.memset("pool", WF, 0.0, writes=[Wb_b])
            B.memset("pool", gq, 0.0, writes=[gq0_b])
            B.memset("pool", VA4[:, :, :, 64:128], 1.0, writes=[VAones_b])
            def pass_weight_steps(pp, slots, slot_bs, keys):
                col0 = [256 * pp, 512 + 256 * pp, 1024 + 256 * pp]
                steps = []
                for i in range(3):
                    def st(i=i):
                        sl = i % 2
                        st3 = slots[sl].rearrange("p (f n) -> p f n", n=256)
                        B.dma(st3, w_in_r[:, :, col0[i]:col0[i] + 256], keys[sl], writes=[slot_bs[sl]])
                        for fc in range(8):
                            B.ts("dve", Wb3[:, fc, 256 * i:256 * i + 256], st3[:, fc, :], gains[:, fc:fc + 1], ALU.mult,
                                 reads=[slot_bs[sl], prm_b], writes=[Wb_b])
                    steps.append(st)

                def stf_():
                    stf = slots[1][:, 0:32].rearrange("p (f n) -> p f n", n=4)
                    B.dma(stf, w_in_r[:, :, 1536 + 4 * pp:1536 + 4 * pp + 4], keys[1], writes=[slot_bs[1]])
                    for fc in range(8):
                        for g in range(3):
                            B.ts("dve", WF3[:, fc, 32 * g:32 * g + 4], stf[:, fc, :], gains[:, fc:fc + 1], ALU.mult,
                                 reads=[slot_bs[1], prm_b], writes=[Wb_b])
                steps.append(stf_)
                return steps

            pre_x = set()
            if p == 0:
                for ch0 in range(2):
                    B.dma(xs[ch0], xc[ch0], "ld_xs%d" % ch0, writes=[xs_b[ch0]])
                    pre_x.add(ch0)
                for st_ in pass_weight_steps(0, [sq.bitcast(F32), xb[1].bitcast(F32)], [sq_b, xb_b[1]], ["ld_w0", "ld_w1"]):
                    st_()

            def stageA1(ch):
                sl = ch % 2
                if ch not in pre_x:
                    B.dma(xs[sl], xc[ch], "ld_xs%d" % sl, writes=[xs_b[sl]])
                B.act(sq, xs[sl], AF.Square, reads=[xs_b[sl]], writes=[sq_b])
                B.cp("dve", xb[sl][:, 0:2048], xs[sl][:, 0:2048], reads=[xs_b[sl]], writes=[xb_b[sl]])
                B.act(xb[sl][:, 2048:4096], xs[sl][:, 2048:4096], AF.Copy, reads=[xs_b[sl]], writes=[xb_b[sl]])

            def stageA1mm(ch):
                sq3 = sq.rearrange("p (f t) -> p f t", t=512)
                for fc in range(8):
                    B.mm(PS[0][:, :], onesS, sq3[:, fc, :], fc == 0, fc == 7, reads=[sq_b, cbuf], writes=[PS_b[0]],
                         sig=(fc == 7))

            def stageA2act(ch):
                sl = ch % 2
                rstd_from_ms(PS[0], PS_b[0], lnt, lnt_b, rstd[sl], rstd_b[sl], 512)

            def stageA2rest(ch):
                sl = ch % 2
                rs = rstd[sl]
                for blk in range(4):
                    B.mm(PS[6][:, sl * 4 + blk:sl * 4 + blk + 1], rs[0:1, blk * 128:(blk + 1) * 128], onesf[0:1, 0:1], True, True,
                         reads=[rstd_b[sl], cbuf], writes=[PS_b[6]], sig=(blk == 3))
                B.cp("dve", rcol[sl], PS[6][:, sl * 4:sl * 4 + 4], reads=[PS_b[6]], writes=[rcol_b[sl]])

            def stageC(ch, do_pe, do_dve):
                cs = ch % 2
                if do_pe:
                    for blk in range(4):
                        B.mm(PS[6][:, 16 + cs * 16 + blk * 4:16 + cs * 16 + blk * 4 + 4], cn[cs][0:4, blk * 128:(blk + 1) * 128], identf[0:4, 0:4],
                             True, True, reads=[cn_b[cs], cbuf], writes=[PS_b[6]], sig=(blk == 3))
                if not do_dve:
                    return
                B.tt("dve", biasK[:, ch * 16:(ch + 1) * 16], PS[6][:, 16 + cs * 16:16 + cs * 16 + 16], kvalid[:, ch * 16:(ch + 1) * 16], ALU.add,
                     reads=[PS_b[6], cbuf], writes=[biasK_b[ch]])
                B.ts("dve", t0[0:72, :], cn[cs][0:72, 384:512], -8.0, ALU.mult, reads=[cn_b[cs]], writes=[t0_b])
                B.cp("dve", hib[0:72, :], t0[0:72, :], reads=[t0_b], writes=[hib_b])
                B.tt("dve", r1[0:72, :], t0[0:72, :], hib[0:72, :], ALU.subtract, reads=[t0_b, hib_b], writes=[r1_b])
                B.cp("dve", midb[0:72, :], r1[0:72, :], reads=[r1_b], writes=[midb_b])
                B.tt("dve", r2[0:72, :], r1[0:72, :], midb[0:72, :], ALU.subtract, reads=[r1_b, midb_b], writes=[r2_b])
                B.cp("dve", gq[0:4, ch * 128:(ch + 1) * 128], hib[0:4, :], reads=[hib_b, gq0_b], writes=[gq_b[ch]])
                B.cp("dve", gq[32:36, ch * 128:(ch + 1) * 128], midb[32:36, :], reads=[midb_b, gq0_b], writes=[gq_b[ch]])
                B.cp("dve", gq[64:68, ch * 128:(ch + 1) * 128], r2[64:68, :], reads=[r2_b, gq0_b], writes=[gq_b[ch]])

            def stageB1(ch):
                sl = ch % 2
                rs = rstd[sl]
                xb3 = xb[sl].rearrange("p (f t) -> p f t", t=512)
                for fc in range(8):
                    B.mm(PS[5][:, :], WF3[:, fc, :], xb3[:, fc, :], fc == 0, fc == 7,
                         reads=[Wb_b, xb_b[sl]], writes=[PS_b[5]], sig=(fc == 7))
                B.tt("dve", fr[0:72, :], PS[5][0:72, :], rs[0:72, :], ALU.mult, reads=[PS_b[5], rstd_b[sl]], writes=[fr_b])
                B.act(ee[0:72, :], fr[0:72, :], AF.Exp, bias=negbf[0:72, p:p + 1], scale=-1.0, reads=[fr_b, cbuf], writes=[ee_b])
                B.act(fr[0:72, :], ee[0:72, :], AF.Ln, bias=onesf[0:72, 0:1], scale=1.0, reads=[ee_b, cbuf], writes=[fr_b])
                cs = ch % 2
                init = 0.0 if ch == 0 else cn[1 - cs][0:72, 511:512]
                B.emit("dve", lambda E, o=cn[cs][0:72, :], d=fr[0:72, :], i=init: E.tensor_tensor_scan(
                    out=o, data0=d, data1=d, initial=i, op0=ALU.add, op1=ALU.bypass),
                    reads=[fr_b, cn_b[1 - cs]], writes=[cn_b[cs]])
                for pr in range(2):
                    bank = 1 + pr
                    for fc in range(8):
                        B.mm(PS[bank][:, :], Wb3[:, fc, 256 + pr * 128:256 + pr * 128 + 128], xb3[:, fc, :], fc == 0, fc == 7,
                             reads=[Wb_b, xb_b[sl]], writes=[PS_b[bank]], sig=(fc == 7))
                    B.tt("dve", KT3[:, pr, ch * 512:(ch + 1) * 512], PS[bank][:, :], rs, ALU.mult,
                         reads=[PS_b[bank], rstd_b[sl]], writes=[KT_b[pr][ch]])

            def stageB2(ch):
                sl = ch % 2
                xb3 = xb[sl].rearrange("p (f t) -> p f t", t=512)
                for blk in range(4):
                    bank = 3 + (blk % 2)
                    kb = ch * 4 + blk
                    for fc in range(8):
                        B.mm(PS[bank][:, 0:256], xb3[:, fc, blk * 128:(blk + 1) * 128], Wb3[:, fc, 512:768], fc == 0, fc == 7,
                             reads=[Wb_b, xb_b[sl]], writes=[PS_b[bank]], sig=(fc == 7))
                    vout = VA4[:, kb, :, :].rearrange("p q (g d) -> p q g d", d=64)[:, :, 0:3:2, :]
                    vin = PS[bank][:, 0:256].rearrange("p (q g d) -> p q g d", g=2, d=64)
                    B.act(vout, vin, AF.Copy, scale=rcol[sl][:, blk:blk + 1],
                          reads=[PS_b[bank], rcol_b[sl]], writes=[VA_b[kb]])

            def stageB3(ch):
                sl = ch % 2
                rs = rstd[sl]
                xb3 = xb[sl].rearrange("p (f t) -> p f t", t=512)
                if ch > 0:
                    stageC(ch - 1, True, False)
                for pr in range(2):
                    for fc in range(8):
                        B.mm(PS[7][:, pr * 128:(pr + 1) * 128], Wb3[:, fc, pr * 128:(pr + 1) * 128], xb3[:, fc, 384:512], fc == 0, fc == 7,
                             reads=[Wb_b, xb_b[sl]], writes=[PS_b[7]], sig=(fc == 7 and pr == 1))
                for pr in range(2):
                    for par in range(2):
                        rr = slice(par * 64, par * 64 + 64)
                        B.tt("dve", QT3[rr, 2 * pr + par, ch * 128:(ch + 1) * 128], PS[7][rr, pr * 128:(pr + 1) * 128], rs[rr, 384:512], ALU.mult,
                             reads=[PS_b[7], rstd_b[sl], QT0_b], writes=[QT_b[pr][ch]])
                if ch > 0:
                    stageC(ch - 1, False, True)

            stageA1(0)
            stageA1mm(0)
            stageA2act(0)
            stageA2rest(0)
            for ch in range(16):
                if ch + 1 < 16:
                    stageA1(ch + 1)
                stageB1(ch)
                if ch + 1 < 16:
                    stageA1mm(ch + 1)
                    stageA2act(ch + 1)
                stageB2(ch)
                if ch + 1 < 16:
                    stageA2rest(ch + 1)
                stageB3(ch)
            stageC(15, True, True)

            def inherit(dst, *srcs):
                for sb_ in srcs:
                    evs_ = list(sb_.r.items()) + ([sb_.w] if sb_.w is not None else [])
                    for k_, n_ in evs_:
                        if dst.r.get(k_, 0) < n_:
                            dst.r[k_] = n_
            for b_ in PT_b + [rden_b] + otmp_b:
                inherit(b_, xs_b[0])
            tiles = []
            for hl in range(4):
                for u in range(4):
                    nk = 16 * u + 16
                    for kb in range(nk):
                        tiles.append((hl, u, kb, kb == nk - 1))
            NT = len(tiles)
            LA = 2
            sev = [None] * NT
            deferred = {}

            def emit_S(i):
                hl, u, kb, last = tiles[i]
                pr, par = hl // 2, hl % 2
                rows = slice(par * 64, par * 64 + 64)
                sb = i % 3
                bank = PS[sb]
                kbl = kb - 16 * u
                mmin = 0 if kbl < 0 else kbl // 4
                c0 = 128 * mmin
                diag = kbl >= 0 and kbl % 4 == 3
                ch = kb // 4
                rd = [KT_b[pr][ch]] + [QT_b[pr][4 * u + m] for m in range(4)] + [gq_b[4 * u + m] for m in range(4)] + [cbuf, gq0_b, QT0_b]
                q0 = u * 512
                lk = KT3[:, pr, kb * 128:(kb + 1) * 128]

                def grp(ca, cb, mask):
                    B.mm(bank[:, ca:cb], lk, QT3[:, hl, q0 + ca:q0 + cb], True, False, reads=rd, writes=[PS_b[sb]])
                    ev = B.mm(bank[:, ca:cb], selb[:, hl * 128:(hl + 1) * 128], gq[:, q0 + ca:q0 + cb], False, not mask,
                              reads=rd, writes=[PS_b[sb]], sig=not mask)
                    if mask:
                        ev = B.mm(bank[:, ca:cb], identb, trib, False, True, reads=rd, writes=[PS_b[sb]], sig=True)
                    return ev
                if not diag:
                    grp(c0, 512, False)
                else:
                    if c0 + 128 < 512:
                        grp(c0 + 128, 512, False)
                    grp(c0, c0 + 128, True)

            def emit_EP(i):
                hl, u, kb, last = tiles[i]
                pr, par = hl // 2, hl % 2
                sb = i % 3
                kbl = kb - 16 * u
                mmin = 0 if kbl < 0 else kbl // 4
                c0 = 128 * mmin
                diag = kbl >= 0 and kbl % 4 == 3
                B.act(PT[sb][:, c0:512], PS[sb][:, c0:512], AF.Exp, bias=biasK[:, kb * 4 + hl:kb * 4 + hl + 1], scale=0.125,
                      reads=[PS_b[sb], biasK_b[kb // 4]], writes=[PT_b[sb]])
                ob = 3 + ((hl * 4 + u) % 2)
                lv = VA4[:, kb, pr, par * 64:par * 64 + 128]
                rdv = [VA_b[kb], VAones_b, PT_b[sb]]
                if not diag:
                    B.mm(PS[ob][:, c0:512], lv, PT[sb][:, c0:512], kb == 0, False, reads=rdv, writes=[PS_b[ob]], sig=True)
                else:
                    if c0 + 128 < 512:
                        B.mm(PS[ob][:, c0 + 128:512], lv, PT[sb][:, c0 + 128:512], kb == 0, False, reads=rdv, writes=[PS_b[ob]], sig=False)
                    B.mm(PS[ob][:, c0:c0 + 128], lv, PT[sb][:, c0:c0 + 128], kb == 0, last, reads=rdv, writes=[PS_b[ob]], sig=True)
                if last:
                    rows = slice(par * 64, par * 64 + 64)
                    r = 64 if par == 0 else 0
                    ot = otmp[(hl * 4 + u) % 2]
                    ot_b = otmp_b[(hl * 4 + u) % 2]
                    B.emit("dve", lambda E, o=rden[r:r + 1, :], s=PS[ob][r:r + 1, :]: E.reciprocal(out=o, in_=s),
                           reads=[PS_b[ob]], writes=[rden_b])
                    B.cp("dve", ot[rows, :], PS[ob][rows, :], reads=[PS_b[ob]], writes=[ot_b])
                    pg = 2 * p + pr

                    def fin(r=r, rows=rows, ot=ot, ot_b=ot_b, pg=pg, u=u):
                        B.mm(PS[5][:, :], onesf[r:r + 1, 0:128], rden[r:r + 1, :], True, True,
                             reads=[rden_b, cbuf], writes=[PS_b[5]], sig=True)
                        B.tt("dve", ybT3[rows, pg, u * 512:(u + 1) * 512], ot[rows, :], PS[5][rows, :], ALU.mult,
                             reads=[ot_b, PS_b[5]], writes=[ybT_b[pg * 4 + u]])
                    deferred[min(i + 6, NT - 1)] = fin

            if p == 0:
                pf_b = [Buf(), Buf()]
                inherit(pf_b[0], xs_b[1])
                inherit(pf_b[1], xs_b[1])
                pend.extend(pass_weight_steps(1, [xs[1][:, 0:2048], xs[1][:, 2048:4096]], pf_b, ["ld_pf0", "ld_pf1"]))
            if p == 1:
                R0 = XS_OFF + 16384
                WdA = V(R0, BF16, 8 * 1536)
                stg_all = V(R0 + 24576, F32, 4096)
                stg = [stg_all[:, 0:2048], stg_all[:, 2048:4096]]
                Wd1 = WdA.rearrange("p (f n) -> p f n", n=1536)
                for b_ in A1_bs:
                    inherit(b_, xs_b[1], xb_b[0])
                inherit(stg_b[0], xb_b[1])
                inherit(stg_b[1], sq_b)
                pend.extend(load_w_steps(Wd1, A1_bs, w_in_r[:, :, 1544:1544 + 1536], 1536, 0, dve_only=True))
            for i in range(min(LA, NT)):
                emit_S(i)
            for i in range(NT):
                if i + LA < NT:
                    emit_S(i + LA)
                emit_EP(i)
                if i in deferred:
                    deferred.pop(i)()
                if i % 64 == 40:
                    pump(1)
            assert not deferred
            flush()
            if p == 1:
                B.dma(stg_all, xo[0], "ld_st0", writes=stg_b)
            B.barrier()

        if DEBUG:
            a = Alloc(BASE, TOP)
            dtmp = a(F32, 4 * 2048)
            db = Buf()
            B.cp("dve", dtmp, ybT, writes=[db])
            B.dma(dbg_yb[:, :], dtmp, "st_dbg", reads=[db])
            B.barrier()

        a = Alloc(BASE, R0)
        t1 = a(BF16, 8 * 2048)
        WdB = a(BF16, 8 * 1536)
        D4_BASE = a.p
        hown = a(BF16, 8 * 2048)
        ya = a(BF16, 4 * 2048)
        sq = a(BF16, 4096)
        lnt = a(F32, 512)
        rstd1 = a(F32, 512)
        hh_ = a(BF16, 8 * 32)
        zh = a(F32, 4 * 32)
        zt = [a(F32, 4 * 130) for _ in range(2)]
        csb = [a(F32, 512) for _ in range(2)]
        a = Alloc(R0 + 40960, TOP)
        cv = [a(F32, 512) for _ in range(2)]
        sg = [a(F32, 512) for _ in range(2)]
        tm = [a(F32, 512) for _ in range(2)]

        t1_b = [[Buf() for _ in range(4)] for _ in range(8)]
        hown_b = [Buf() for _ in range(4)]
        ya_b = [[Buf() for _ in range(4)] for _ in range(4)]
        sq_b, lnt_b, rstd1_b, hh_b, zh_b = (Buf() for _ in range(5))
        zt_b = [Buf(), Buf()]
        csb_b = [Buf(), Buf()]
        cv_b = [Buf(), Buf()]
        sg_b = [Buf(), Buf()]
        tm_b = [Buf(), Buf()]
        PS_b = [Buf() for _ in range(8)]
        Abr_bs = [Buf() for _ in range(4)]
        Ag_bs = [Buf() for _ in range(8)]
        Bbr_bs = [Buf() for _ in range(4)]
        Bg_bs = [Buf() for _ in range(8)]
        Wo_bs = [Buf() for _ in range(8)]

        t13 = t1.rearrange("p (f t) -> p f t", t=2048)
        hown3 = hown.rearrange("p (f t) -> p f t", t=2048)
        ya3 = ya.rearrange("p (f t) -> p f t", t=2048)
        ybT3 = ybT.rearrange("p (q t) -> p q t", t=2048)
        sq3 = sq.rearrange("p (f t) -> p f t", t=512)

        def norm_stats(src3, src_rd, ncols):
            sqv = sq[:, 0:8 * ncols].rearrange("p (f t) -> p f t", t=ncols)
            B.act(sqv, src3, AF.Square, reads=src_rd, writes=[sq_b])
            for fc in range(8):
                B.mm(PS[0][:, 0:ncols], onesS, sqv[:, fc, :], fc == 0, fc == 7, reads=[sq_b, cbuf], writes=[PS_b[0]], sig=(fc == 7))
            rstd_from_ms(PS[0], PS_b[0], lnt, lnt_b, rstd1, rstd1_b, ncols)

        xsB = WdB[:, 0:8192].bitcast(F32)
        xsB_b = Buf()
        xhs = WdB[:, 8192:8192 + 512].bitcast(F32)
        xhs_b = Buf()
        B.dma(xhs, xh[:, :], "ld_xh", writes=[xhs_b])
        B.dma(xsB, xo[1], "ld_xsB", writes=[xsB_b])
        hh3 = hh_.rearrange("p (f t) -> p f t", t=32)
        def prelude_steps(tc):
            if tc % 2 == 0:
                xst, xst_bs, key = stg_all, stg_b, "ld_st0"
            else:
                xst, xst_bs, key = xsB, [xsB_b], "ld_xsB"
            xs03 = xst.rearrange("p (f t) -> p f t", t=512)

            def sA():
                if tc >= 2:
                    B.dma(xst, xo[tc], key, writes=xst_bs)
                norm_stats(xs03, xst_bs, 512)

            def mk(f0):
                def sB():
                    for fc in range(f0, f0 + 4):
                        B.tt("dve", hown3[:, fc, tc * 512:(tc + 1) * 512], xs03[:, fc, :], rstd1, ALU.mult,
                             reads=xst_bs + [rstd1_b], writes=[hown_b[tc]])
                return sB
            return [sA, mk(0), mk(4)]

        for st_ in prelude_steps(0):
            st_()
        xh3 = xhs.rearrange("p (f t) -> p f t", t=32)
        norm_stats(xh3, [xhs_b], 32)
        for fc in range(8):
            B.tt("dve", hh3[:, fc, :], xh3[:, fc, :], rstd1[:, 0:32], ALU.mult, reads=[xhs_b, rstd1_b], writes=[hh_b])

        Wd1 = WdA.rearrange("p (f n) -> p f n", n=1536)
        Abr3 = WdA[:, 0:4096].rearrange("p (f n) -> p f n", n=1024)
        Ag3 = WdA[:, 4096:4096 + 8192].rearrange("p (f n) -> p f n", n=1024)
        Bbr3 = WdB[:, 0:4096].rearrange("p (f n) -> p f n", n=1024)
        Bg3 = WdB[:, 4096:4096 + 8192].rearrange("p (f n) -> p f n", n=1024)
        Wo3 = WdB[:, 0:8192].rearrange("p (f n) -> p f n", n=1024)
        w_br_r = [w_br[i].rearrange("(fc p) n -> p fc n", p=128) for i in range(2)]

        g2 = load_w_steps(Bbr3, Bbr_bs, w_br_r[0], 1024, None, extra_w=[xsB_b, xhs_b]) + \
            load_w_steps(Bg3, Bg_bs, w_in_r[:, :, 3080:3080 + 1024], 1024, 0, extra_w=[xsB_b, xhs_b])
        pend.extend(prelude_steps(1) + prelude_steps(2) + prelude_steps(3) + g2)

        zh3 = zh.rearrange("p (f t) -> p f t", t=32)
        for fcw in range(4):
            for which, bank in ((1, 1), (2, 2)):
                for fc in range(8):
                    B.mm(PS[bank][:, 0:32], Wd1[:, fc, which * 512 + fcw * 128:which * 512 + fcw * 128 + 128], hh3[:, fc, :], fc == 0, fc == 7,
                         reads=[A1_bs[fc], hh_b], writes=[PS_b[bank]], sig=(fc == 7))
            B.cp("dve", csb[0][:, 0:32], PS[1][:, 0:32], reads=[PS_b[1]], writes=[csb_b[0]])
            B.tt("dve", zh3[:, fcw, :], csb[0][:, 0:32], PS[2][:, 0:32], ALU.mult, reads=[csb_b[0], PS_b[2]], writes=[zh_b])
        it = 0
        for tc in range(4):
            for fcw in range(4):
                s2 = it % 2
                it += 1
                banks = (1 + 3 * s2, 2 + 3 * s2, 3 + 3 * s2)
                for which in range(3):
                    bank = banks[which]
                    for fc in range(8):
                        B.mm(PS[bank][:, :], Wd1[:, fc, which * 512 + fcw * 128:which * 512 + fcw * 128 + 128], hown3[:, fc, tc * 512:(tc + 1) * 512],
                             fc == 0, fc == 7, reads=[A1_bs[fc], hown_b[tc]], writes=[PS_b[bank]], sig=(fc == 7))
                z3 = zt[s2].rearrange("p (b t) -> p b t", t=130)
                B.act(csb[s2], PS[banks[1]][:, :], AF.Copy, reads=[PS_b[banks[1]]], writes=[csb_b[s2]])
                B.cp("dve", z3[:, :, 0:2], zh3[:, fcw, tc * 8:(tc + 1) * 8].rearrange("p (b t) -> p b t", t=2), reads=[zh_b], writes=[zt_b[s2]])
                B.tt("dve", z3[:, :, 2:130], csb[s2].rearrange("p (b t) -> p b t", t=128), PS[banks[2]][:, :].rearrange("p (b t) -> p b t", t=128),
                     ALU.mult, reads=[csb_b[s2], PS_b[banks[2]]], writes=[zt_b[s2]])
                cv3 = cv[s2].rearrange("p (b t) -> p b t", t=128)
                B.ts("dve", cv3, z3[:, :, 0:128], convw[:, fcw * 3:fcw * 3 + 1], ALU.mult, reads=[zt_b[s2], prm_b], writes=[cv_b[s2]])
                B.stt("dve", cv3, z3[:, :, 1:129], convw[:, fcw * 3 + 1:fcw * 3 + 2], cv3, ALU.mult, ALU.add,
                      reads=[zt_b[s2], prm_b, cv_b[s2]], writes=[cv_b[s2]])
                B.stt("dve", cv3, z3[:, :, 2:130], convw[:, fcw * 3 + 2:fcw * 3 + 3], cv3, ALU.mult, ALU.add,
                      reads=[zt_b[s2], prm_b, cv_b[s2]], writes=[cv_b[s2]])
                B.tt("dve", ya3[:, fcw, tc * 512:(tc + 1) * 512], cv[s2], PS[banks[0]][:, :], ALU.mult,
                     reads=[cv_b[s2], PS_b[banks[0]]], writes=[ya_b[fcw][tc]])
                pump(2)
        flush()

        for br in range(2):
            if br == 0:
                Wbr3, Wbr_bs, Wg3, Wg_bs = Bbr3, Bbr_bs, Bg3, Bg_bs
                pend.extend(load_w_steps(Abr3, Abr_bs, w_br_r[1], 1024, None, extra_w=A1_bs))
                pend.extend(load_w_steps(Ag3, Ag_bs, w_in_r[:, :, 4104:4104 + 1024], 1024, 0, extra_w=A1_bs))
            else:
                Wbr3, Wbr_bs, Wg3, Wg_bs = Abr3, Abr_bs, Ag3, Ag_bs
                pend.extend(load_w_steps(Wo3, Wo_bs, w_out_r, 1024, None, extra_w=Bbr_bs + Bg_bs))
            it = 0
            for tc in range(4):
                for fo in range(8):
                    s2 = it % 2
                    it += 1
                    bg, bb = 1 + 2 * s2, 2 + 2 * s2
                    for fc in range(8):
                        B.mm(PS[bg][:, :], Wg3[:, fc, fo * 128:(fo + 1) * 128], hown3[:, fc, tc * 512:(tc + 1) * 512], fc == 0, fc == 7,
                             reads=[Wg_bs[fc], hown_b[tc]], writes=[PS_b[bg]], sig=(fc == 7))
                    for fcw in range(4):
                        if br == 0:
                            rhs, rb = ya3[:, fcw, tc * 512:(tc + 1) * 512], ya_b[fcw][tc]
                        else:
                            rhs, rb = ybT3[:, fcw, tc * 512:(tc + 1) * 512], ybT_b[fcw * 4 + tc]
                        B.mm(PS[bb][:, :], Wbr3[:, fcw, fo * 128:(fo + 1) * 128], rhs, fcw == 0, fcw == 3,
                             reads=[Wbr_bs[fcw], rb], writes=[PS_b[bb]], sig=(fcw == 3))
                    B.act(sg[s2], PS[bg][:, :], AF.Sigmoid, reads=[PS_b[bg]], writes=[sg_b[s2]])
                    if br == 0:
                        B.tt("dve", t13[:, fo, tc * 512:(tc + 1) * 512], sg[s2], PS[bb][:, :], ALU.mult,
                             reads=[sg_b[s2], PS_b[bb]], writes=[t1_b[fo][tc]])
                    else:
                        B.tt("dve", tm[s2], sg[s2], PS[bb][:, :], ALU.mult, reads=[sg_b[s2], PS_b[bb]], writes=[tm_b[s2]])
                        B.tt("dve", t13[:, fo, tc * 512:(tc + 1) * 512], tm[s2], t13[:, fo, tc * 512:(tc + 1) * 512], ALU.add,
                             reads=[tm_b[s2], t1_b[fo][tc]], writes=[t1_b[fo][tc]])
                    if it % 2 == 0:
                        pump(1)
            flush()
        B.barrier()

        X1_OFF = TOP - 8 * 2048 * 4
        x1 = V(X1_OFF, F32, 8 * 2048)
        x13 = x1.rearrange("p (f t) -> p f t", t=2048)
        x1_b = [Buf() for _ in range(4)]
        a = Alloc(D4_BASE, X1_OFF)
        msbs = [a(F32, 4096) for _ in range(2)]
        sqs = [a(BF16, 4096) for _ in range(2)]
        lnt = a(F32, 512)
        rstd1 = a(F32, 512)
        tm = [a(F32, 512) for _ in range(2)]
        msbs_b = [Buf(), Buf()]
        sqs_b = [Buf(), Buf()]
        lnt_b, rstd1_b = Buf(), Buf()
        tm_b = [Buf(), Buf()]
        PS_b = [Buf() for _ in range(8)]
        for tc in range(4):
            for fc in range(8):
                B.dma(x13[:, fc, tc * 512:(tc + 1) * 512], xo[tc][:, fc * 512:(fc + 1) * 512], "ld_x1_%d" % tc, writes=[x1_b[tc]])

        def post_norm_residual(tc_glob, fsb3, fsb_b, gain_off):
            for fc in range(8):
                B.mm(PS[0][:, :], onesS, sq3[:, fc, :], fc == 0, fc == 7, reads=[sq_b, cbuf], writes=[PS_b[0]], sig=(fc == 7))
            rstd_from_ms(PS[0], PS_b[0], lnt, lnt_b, rstd1, rstd1_b, 512)
            for fo in range(8):
                s2 = fo % 2
                B.tt("dve", tm[s2], fsb3[:, fo, :], rstd1, ALU.mult, reads=[fsb_b, rstd1_b], writes=[tm_b[s2]])
                xsl = x13[:, fo, tc_glob * 512:(tc_glob + 1) * 512]
                B.stt("dve", xsl, tm[s2], gains[:, gain_off + fo:gain_off + fo + 1], xsl, ALU.mult, ALU.add,
                      reads=[tm_b[s2], prm_b, x1_b[tc_glob]], writes=[x1_b[tc_glob]])

        it = 0
        for tc in range(4):
            msb3 = msbs[tc % 2].rearrange("p (f t) -> p f t", t=512)
            msb_b = msbs_b[tc % 2]
            sq3 = sqs[tc % 2].rearrange("p (f t) -> p f t", t=512)
            sq_b = sqs_b[tc % 2]
            for fo in range(8):
                bank = 1 + it % 4
                it += 1
                for fc in range(8):
                    B.mm(PS[bank][:, :], Wo3[:, fc, fo * 128:(fo + 1) * 128], t13[:, fc, tc * 512:(tc + 1) * 512], fc == 0, fc == 7,
                         reads=[Wo_bs[fc], t1_b[fc][tc]], writes=[PS_b[bank]], sig=(fc == 7))
                B.act(msb3[:, fo, :], PS[bank][:, :], AF.Copy, reads=[PS_b[bank]], writes=[msb_b])
                B.act(sq3[:, fo, :], PS[bank][:, :], AF.Square, reads=[PS_b[bank]], writes=[sq_b])
            post_norm_residual(tc, msb3, msb_b, 8)
        B.barrier()

        if DEBUG:
            B.dma(dbg_x1[:, :], x1, "st_dbg", reads=x1_b)
            B.barrier()

        out_evs = []
        for half in range(2):
            a = Alloc(BASE0, X1_OFF)
            aT = a(BF16, 22 * 1024)
            E1_BASE = a.p
            hfT = a(BF16, 8 * 1024)
            stg2 = [a(F32, 4096) for _ in range(2)]
            wgb = [a(BF16, 4096) for _ in range(2)]
            SQ_OFF = a.p
            sq = a(BF16, 4096)
            lnt = a(F32, 512)
            rstd1 = a(F32, 512)
            SG_OFF = a.p
            sg = [a(F32, 512) for _ in range(2)]
            FREE_OFF = a.p
            stg3 = V(SQ_OFF, F32, 11 * 256)
            wdb = [V(FREE_OFF, BF16, 22 * 256), None]
            stg3_b = Buf()
            wdb_b = [[Buf(), Buf()] for _ in range(2)]
            st3 = stg3.rearrange("p (h n) -> p h n", n=256)
            aT_b = [[Buf() for _ in range(2)] for _ in range(22)]
            hfT_b = [Buf(), Buf()]
            stg2_b = [Buf(), Buf()]
            wgb_b = [[Buf() for _ in range(8)] for _ in range(2)]
            sq_b, lnt_b, rstd1_b = Buf(), Buf(), Buf()
            sg_b = [Buf(), Buf()]
            PS_b = [Buf() for _ in range(8)]
            aT3 = aT.rearrange("p (h t) -> p h t", t=1024)
            hfT3 = hfT.rearrange("p (f t) -> p f t", t=1024)
            sq3 = sq.rearrange("p (f t) -> p f t", t=512)

            def gu_steps(hb):
                sl = hb % 2
                st4 = stg2[sl].rearrange("p (f g n) -> p f g n", g=2, n=256)

                def s0():
                    for g in range(2):
                        B.dma(st4[:, :, g, :], w_gu_r[:, :, g * FFN + hb * 256:g * FFN + hb * 256 + 256], "ld_sg%d" % sl, writes=[stg2_b[sl]])

                def mk(fc):
                    def s():
                        if fc % 2:
                            B.act(wgb[sl][:, fc * 512:(fc + 1) * 512], stg2[sl][:, fc * 512:(fc + 1) * 512], AF.Copy,
                                  scale=gains[:, 16 + fc:16 + fc + 1], reads=[stg2_b[sl], prm_b], writes=[wgb_b[sl][fc]])
                        else:
                            B.ts("dve", wgb[sl][:, fc * 512:(fc + 1) * 512], stg2[sl][:, fc * 512:(fc + 1) * 512],
                                 gains[:, 16 + fc:16 + fc + 1], ALU.mult, reads=[stg2_b[sl], prm_b], writes=[wgb_b[sl][fc]])
                    return s
                return [s0] + [mk(fc) for fc in range(8)]

            def dn_steps(cb):
                sl = cb % 2
                wd3 = wdb[sl].rearrange("p (h n) -> p h n", n=256)

                def mk(hh2):
                    def s():
                        ex = [e0_sq_b, e0_lnt_b, e0_rstd_b] if (cb == 0 and hh2 == 0) else []
                        B.dma(st3, w_dn_r[:, hh2 * 11:(hh2 + 1) * 11, cb * 256:(cb + 1) * 256], "ld_s3", writes=[stg3_b] + ex)
                        B.cp("dve", wd3[:, hh2 * 11:hh2 * 11 + 6, :], st3[:, 0:6, :], reads=[stg3_b], writes=[wdb_b[sl][hh2]])
                        B.act(wd3[:, hh2 * 11 + 6:hh2 * 11 + 11, :], st3[:, 6:11, :], AF.Copy, reads=[stg3_b], writes=[wdb_b[sl][hh2]])
                    return s
                return [mk(0), mk(1)]

            e0_sq_b, e0_lnt_b, e0_rstd_b = sq_b, lnt_b, rstd1_b
            pend.extend(gu_steps(0))
            pump(1)
            for tcc in range(2):
                tg = half * 2 + tcc
                B.act(sq3, x13[:, :, tg * 512:(tg + 1) * 512], AF.Square, reads=[x1_b[tg]], writes=[sq_b])
                for fc in range(8):
                    B.mm(PS[0][:, :], onesS, sq3[:, fc, :], fc == 0, fc == 7, reads=[sq_b, cbuf], writes=[PS_b[0]], sig=(fc == 7))
                rstd_from_ms(PS[0], PS_b[0], lnt, lnt_b, rstd1, rstd1_b, 512)
                if tcc == 0:
                    pump(8)
                for fc in range(8):
                    B.tt("dve", hfT3[:, fc, tcc * 512:(tcc + 1) * 512], x13[:, fc, tg * 512:(tg + 1) * 512], rstd1, ALU.mult,
                         reads=[x1_b[tg], rstd1_b], writes=[hfT_b[tcc]])
                pump(5)
            flush()
            it = 0
            for hb in range(11):
                sl = hb % 2
                wg4 = wgb[sl].rearrange("p (f g n) -> p f g n", g=2, n=256)
                if hb + 1 < 11:
                    pend.extend(gu_steps(hb + 1))
                    pump(1)
                if hb == 10:
                    pend.extend(dn_steps(0))
                for tcc in range(2):
                    for hc2 in range(2):
                        s2 = it % 2
                        it += 1
                        bg, bu = 1 + 2 * s2, 2 + 2 * s2
                        for g, bank in ((0, bg), (1, bu)):
                            for fc in range(8):
                                B.mm(PS[bank][:, :], wg4[:, fc, g, hc2 * 128:(hc2 + 1) * 128], hfT3[:, fc, tcc * 512:(tcc + 1) * 512], fc == 0, fc == 7,
                                     reads=[wgb_b[sl][fc], hfT_b[tcc]], writes=[PS_b[bank]], sig=(fc == 7))
                        B.act(sg[s2], PS[bg][:, :], AF.Silu, reads=[PS_b[bg]], writes=[sg_b[s2]])
                        B.tt("dve", aT3[:, hb * 2 + hc2, tcc * 512:(tcc + 1) * 512], sg[s2], PS[bu][:, :], ALU.mult,
                             reads=[sg_b[s2], PS_b[bu]], writes=[aT_b[hb * 2 + hc2][tcc]])
                        pump(2)
                flush()
            B.barrier()
            a = Alloc(E1_BASE, SQ_OFF)
            fsb = a(F32, 2 * 4096)
            sqd = a(BF16, 2 * 4096)
            wdb[1] = a(BF16, 22 * 256)
            lnt = a(F32, 512)
            rstd1 = a(F32, 512)
            tm = [V(SG_OFF, F32, 512), V(SG_OFF + 2048, F32, 512)]
            fsb_b = [Buf(), Buf()]
            sqd_b = [Buf(), Buf()]
            lnt_b, rstd1_b = Buf(), Buf()
            tm_b = [Buf(), Buf()]
            PS_b = [Buf() for _ in range(8)]
            fsb4 = fsb.rearrange("p (c f t) -> p c f t", f=8, t=512)
            sqd4 = sqd.rearrange("p (c f t) -> p c f t", f=8, t=512)
            it = 0
            for cb in range(4):
                sl = cb % 2
                wd3 = wdb[sl].rearrange("p (h n) -> p h n", n=256)
                if cb + 1 < 4:
                    pend.extend(dn_steps(cb + 1))
                for tcc in range(2):
                    for fo2 in range(2):
                        fo = cb * 2 + fo2
                        bank = 1 + it % 4
                        it += 1
                        for hc in range(22):
                            B.mm(PS[bank][:, :], wd3[:, hc, fo2 * 128:(fo2 + 1) * 128], aT3[:, hc, tcc * 512:(tcc + 1) * 512], hc == 0, hc == 21,
                                 reads=[wdb_b[sl][hc // 11], aT_b[hc][tcc]], writes=[PS_b[bank]], sig=(hc == 21))
                        B.act(fsb4[:, tcc, fo, :], PS[bank][:, :], AF.Copy, reads=[PS_b[bank]], writes=[fsb_b[tcc]])
                        B.act(sqd4[:, tcc, fo, :], PS[bank][:, :], AF.Square, reads=[PS_b[bank]], writes=[sqd_b[tcc]])
                        if fo2 == 0:
                            pump(1)
                    if cb == 3:
                        tg = half * 2 + tcc
                        sq3 = sqd4[:, tcc]
                        sq_b = sqd_b[tcc]
                        post_norm_residual(tg, fsb4[:, tcc], fsb_b[tcc], 24)
                        for fc in range(8):
                            out_evs.append(B.dma(y[tg][:, fc * 512:(fc + 1) * 512], x13[:, fc, tg * 512:(tg + 1) * 512], "st_y", reads=[x1_b[tg]]))
                flush()
            B.barrier(exclude=("st_y",))


        for ev in out_evs[-1:]:
            B._wait("sp", ev)
        B.ops["sp"].append(lambda E: E.wait_ge(B.sem["st_y"], B.cnt["st_y"]))
        if DEBUG:
            B.ops["sp"].append(lambda E: E.wait_ge(B.sem["st_dbg"], B.cnt["st_dbg"]))

        with nc.Block() as block:
            @block.tensor
            def _(E):
                for f in B.ops["pe"]:
                    f(E)

            @block.scalar
            def _(E):
                for f in B.ops["act"]:
                    f(E)

            @block.vector
            def _(E):
                for f in B.ops["dve"]:
                    f(E)

            @block.gpsimd
            def _(E):
                for f in B.ops["pool"]:
                    f(E)

            @block.sync
            def _(E):
                for f in B.ops["sp"]:
                    f(E)
    return nc


def _relayout_T(xt):
    n = xt.shape[0] // 512
    return np.ascontiguousarray(xt.reshape(n, 512, 8, 128).transpose(0, 3, 2, 1)).reshape(n, 128, 8 * 512)


def make_inputs(x, norm_mix_pre, norm_mix_post, w_in, b_f, conv_w, w_branch, w_out,
                norm_ffn_pre, norm_ffn_post, w_gate_up, w_down):
    f32 = np.float32
    x = np.asarray(x, f32)
    gains = np.stack([np.asarray(g, f32)[0] for g in (norm_mix_pre, norm_mix_post, norm_ffn_pre, norm_ffn_post)])
    gains_l = np.ascontiguousarray(gains.reshape(4, 8, 128).transpose(2, 0, 1)).reshape(128, 32)
    cw = np.asarray(conv_w, f32)[0]
    cw_l = np.ascontiguousarray(cw.reshape(3, 4, 128).transpose(2, 1, 0)).reshape(128, 12)
    bf = np.asarray(b_f, f32)[0]
    ident = np.eye(128, dtype=f32)
    kk = np.arange(128)[:, None]
    qq = np.arange(128)[None, :]
    tri = np.where(qq >= kk, 0.0, NEG).astype(f32)
    sel = np.zeros((128, 8, 128), f32)
    for h in range(8):
        for g in range(3):
            sel[32 * g + (h % 4), h] = 1.0
    sel = sel.reshape(128, 1024)
    w_in0 = np.ascontiguousarray(np.asarray(w_in, f32)[0])
    w_br0 = np.ascontiguousarray(np.asarray(w_branch, f32)[0])
    w_out0 = np.ascontiguousarray(np.asarray(w_out, f32)[0])
    w_gu0 = np.ascontiguousarray(np.asarray(w_gate_up, f32)[0])
    w_dn0 = np.ascontiguousarray(np.asarray(w_down, f32)[0])
    in_maps = []
    for c in range(8):
        b, j = c // 4, c % 4
        npad = (3 - j) * 128
        ctx = np.zeros((SEQ, D), f32)
        ctx[npad:] = x[b, :SEQ - npad]
        own_blocks = [ctx[(4 * s + 3) * 128:(4 * s + 4) * 128] for s in range(16)]
        own = np.concatenate(own_blocks, 0)
        halo = np.concatenate([ctx[(4 * s + 3) * 128 - 2:(4 * s + 3) * 128] for s in range(16)], 0)
        xh = np.ascontiguousarray(halo.reshape(32, 8, 128).transpose(2, 1, 0)).reshape(128, 256)
        prm = np.zeros((128, 48), f32)
        prm[:, 0:32] = gains_l
        prm[:, 32:44] = cw_l
        kval = np.zeros((128, 64, 4), f32)
        kval[:, :3 - j, :] = NEG
        cst = np.concatenate([ident, tri, sel, kval.reshape(128, 256)], 1)
        in_maps.append(dict(xc=_relayout_T(ctx), xo=_relayout_T(own), xh=xh, w_in=w_in0, w_br=w_br0, w_out=w_out0,
                            w_gu=w_gu0, w_dn=w_dn0, prm=prm, cst=cst, _bf=bf))
    return in_maps


_NC = None


def kernel(x, norm_mix_pre, norm_mix_post, w_in, b_f, conv_w, w_branch, w_out,
           norm_ffn_pre, norm_ffn_post, w_gate_up, w_down):
    global _NC
    in_maps = make_inputs(x, norm_mix_pre, norm_mix_post, w_in, b_f, conv_w, w_branch, w_out,
                          norm_ffn_pre, norm_ffn_post, w_gate_up, w_down)
    if _NC is None:
        _NC = build_program()
    for m in in_maps:
        bf = m.pop("_bf")
        for pss in range(2):
            for g in range(3):
                m["prm"][32 * g:32 * g + 4, 44 + pss] = bf[4 * pss:4 * pss + 4]
    res = run_bass_kernel_spmd(_NC, in_maps, core_ids=list(range(8)))
    out = np.zeros((NB, SEQ, D), np.float32)
    for c in range(8):
        b, j = c // 4, c % 4
        yc = res.results[c]["y"].reshape(4, 128, 8, 512)
        own = yc.transpose(0, 3, 2, 1).reshape(2048, D)
        for s in range(16):
            blk = 4 * s + j
            out[b, blk * 128:(blk + 1) * 128] = own[s * 128:(s + 1) * 128]
    kernel.last_results = res
    return out
```
